# Optimizing a Trainium2 kernel written in Bass

```python
import jax
import jax.numpy as jnp
from jax import lax
import numpy as np

D_MODEL = 1024
BATCH = 4
SEQ = 4096
DEPTH = 4

GRID_W = 64
CTX_LEN = 256
HEAD_DIM = 64
ATTN_HEADS = 8
ATTN_KV_HEADS = 2
ATTN_GROUP = ATTN_HEADS // ATTN_KV_HEADS
ATTN_W = ATTN_HEADS * HEAD_DIM
ATTN_KV_W = ATTN_KV_HEADS * HEAD_DIM
RWKV_HEADS = 4
RWKV_W = RWKV_HEADS * HEAD_DIM
DECAY_RANK = 64
ICLR_RANK = 64
GATE_RANK = 128
CONV_GROUPS = 4
CONV_W = CONV_GROUPS * 64
MIX_W = ATTN_W + RWKV_W + CONV_W
ATTN_IN = ATTN_W + 2 * ATTN_KV_W
RWKV_IN = 3 * RWKV_W + 2 * DECAY_RANK + 2 * ICLR_RANK + GATE_RANK
CONV_IN = 3 * CONV_W
N_IN = ATTN_IN + RWKV_IN + CONV_IN
RWKV_SPLITS = (RWKV_W, 2 * RWKV_W, 3 * RWKV_W, 3 * RWKV_W + 2 * DECAY_RANK, 3 * RWKV_W + 2 * DECAY_RANK + 2 * ICLR_RANK)
D_FF = 2816
Q_BLOCK = 128
ROPE_THETA = 10000.0
ROPE_AXIS_PAIRS = HEAD_DIM // 4
DECAY_SCALE = 0.606531
NORM_EPS = 1e-6
GN_EPS = 64e-5
N_MOD = 6

kernel_name = 'hybrid_diffusion_trunk'


def rmsnorm(x, g):
    xf = x.astype(jnp.float32)
    y = xf * lax.rsqrt(jnp.mean(xf * xf, axis=-1, keepdims=True) + NORM_EPS)
    return (y * g.astype(jnp.float32)).astype(x.dtype)


def modulate(h, shift, scale):
    return h * (1 + scale[:, None, :]) + shift[:, None, :]


def neighbours(z):
    zp = jnp.pad(z, ((0, 0), (1, 1), (0, 0)))
    return zp[:, :-2], zp[:, 2:]


def dwconv3(z, w):
    prev, nxt = neighbours(z)
    return prev * w[0] + z * w[1] + nxt * w[2]


def token_shift(z, mu):
    prev, nxt = neighbours(z)
    return z + mu[0] * (prev - z) + mu[1] * (nxt - z)


def axial_angles(T):
    rows = T // GRID_W
    row = jnp.broadcast_to(jnp.arange(rows)[:, None], (rows, GRID_W)).reshape(-1).astype(jnp.float32)
    col = jnp.broadcast_to(jnp.arange(GRID_W)[None, :], (rows, GRID_W)).reshape(-1).astype(jnp.float32)
    inv = ROPE_THETA ** (-jnp.arange(ROPE_AXIS_PAIRS, dtype=jnp.float32) / ROPE_AXIS_PAIRS)
    ang = jnp.concatenate([row[:, None] * inv, col[:, None] * inv], axis=-1)
    return jnp.cos(ang), jnp.sin(ang)


def rope_2d(x, cos, sin):
    xf = x.astype(jnp.float32)
    half = HEAD_DIM // 2
    x1, x2 = xf[..., :half], xf[..., half:]
    cs, sn = cos[None, :, None, :], sin[None, :, None, :]
    return jnp.concatenate([x1 * cs - x2 * sn, x2 * cs + x1 * sn], axis=-1).astype(x.dtype)


def attn_heads(z, q_norm, k_norm):
    B, T, _ = z.shape
    q, k, v = jnp.split(z, [ATTN_W, ATTN_W + ATTN_KV_W], axis=-1)
    q = rmsnorm(q.reshape(B, T, ATTN_HEADS, HEAD_DIM), q_norm)
    k = rmsnorm(k.reshape(B, T, ATTN_KV_HEADS, HEAD_DIM), k_norm)
    v = v.reshape(B, T, ATTN_KV_HEADS, HEAD_DIM)
    return q, k, v


def gqa(q, k, v):
    B, Tq = q.shape[:2]
    qg = q.reshape(B, Tq, ATTN_KV_HEADS, ATTN_GROUP, HEAD_DIM)
    s = jnp.einsum('bqkgd,bskd->bkgqs', qg, k).astype(jnp.float32) * (HEAD_DIM ** -0.5)
    p = jax.nn.softmax(s, axis=-1).astype(v.dtype)
    o = jnp.einsum('bkgqs,bskd->bqkgd', p, v)
    return o.reshape(B, Tq, ATTN_W)


def gqa_blocked(q, k, v):
    B, T = q.shape[:2]
    nb = T // Q_BLOCK
    qb = q.reshape(B, nb, Q_BLOCK, ATTN_HEADS, HEAD_DIM).swapaxes(0, 1)
    o = lax.map(lambda qi: gqa(qi, k, v), qb)
    return o.swapaxes(0, 1).reshape(B, T, ATTN_W)


def rwkv_heads(z):
    return z.reshape(*z.shape[:-1], RWKV_HEADS, HEAD_DIM)


def rwkv_inputs(z, mu, w0, w_up, a0, a_up, g_up, k_k, k_a):
    B, T, _ = z.shape
    z = token_shift(z, mu).astype(jnp.float32)
    r, k, v, wd, ad, gd = jnp.split(z, list(RWKV_SPLITS), axis=-1)
    wd = wd.reshape(B, T, 2, DECAY_RANK)
    ad = ad.reshape(B, T, 2, ICLR_RANK)
    w = jnp.exp(-DECAY_SCALE * jax.nn.sigmoid(w0[:, None, None, :] + jnp.einsum('btdr,drc->dbtc', jnp.tanh(wd), w_up)))
    a = jax.nn.sigmoid(a0[:, None, None, :] + jnp.einsum('btdr,drc->dbtc', ad, a_up))
    g = jax.nn.sigmoid(gd) @ g_up
    kk = rwkv_heads(k * k_k)
    kk = kk * lax.rsqrt(jnp.maximum(jnp.sum(kk * kk, axis=-1, keepdims=True), 1e-12))
    k_eff = k[None] * (1 + (a - 1) * k_a)
    kka = kk[None] * rwkv_heads(a)
    return (rwkv_heads(r), rwkv_heads(w), rwkv_heads(k_eff), rwkv_heads(v), kk, kka, g)


def dir_stack(x):
    x = jnp.stack([x[0], jnp.flip(x[1], axis=1)])
    return jnp.moveaxis(x, 2, 0)


def shared_stack(x):
    return dir_stack(jnp.stack([x, x]))


def dir_merge(y):
    y = jnp.moveaxis(y, 0, 2)
    return y[0] + jnp.flip(y[1], axis=1)


def scan_pack(r, w, k, v, kk, kka):
    return (shared_stack(r), dir_stack(w), dir_stack(k), shared_stack(v), shared_stack(kk), dir_stack(kka))


def rwkv_scan(s0, r, w, k, v, kk, kka):
    def step(S, inp):
        r_t, w_t, k_t, v_t, kk_t, kka_t = inp
        S = (S * w_t[..., None, :]
             - jnp.einsum('...vk,...k->...v', S, kk_t)[..., :, None] * kka_t[..., None, :]
             + v_t[..., :, None] * k_t[..., None, :])
        return S, jnp.einsum('...vk,...k->...v', S, r_t)
    return lax.scan(step, s0, (r, w, k, v, kk, kka))


def rwkv_output(y, r, k_eff, v, g, r_k, lnx_w, lnx_b):
    B, T = y.shape[:2]
    mean = jnp.mean(y, axis=-1, keepdims=True)
    var = jnp.mean(jnp.square(y - mean), axis=-1, keepdims=True)
    yn = ((y - mean) * lax.rsqrt(var + GN_EPS)).reshape(B, T, RWKV_W) * lnx_w + lnx_b
    bonus = (jnp.sum(r[None] * k_eff * r_k, axis=-1, keepdims=True).sum(0) * v).reshape(B, T, RWKV_W)
    return (yn + bonus) * g


def short_conv(z, w):
    bg, cg, xc = jnp.split(z, 3, axis=-1)
    return bg * dwconv3(cg * xc, w)


def conv_ffn(h, w_up, w_conv, w_down):
    u = dwconv3(h @ w_up, w_conv)
    ga, up = jnp.split(u, 2, axis=-1)
    return (jax.nn.silu(ga) * up) @ w_down


def setup_inputs(seed: int = 0) -> dict:
    key = jax.random.key(seed)
    ks = jax.random.split(key, 32)
    L, D = DEPTH, D_MODEL
    f32 = jnp.float32

    def nrm(k, shape, scale):
        return jax.random.normal(k, shape, f32) * scale

    def gain(k, shape):
        return 1.0 + 0.05 * jax.random.normal(k, shape, f32)

    return {
        'x': nrm(ks[0], (BATCH, SEQ, D), 1.0),
        'c': nrm(ks[1], (BATCH, D), 1.0),
        'ctx': nrm(ks[2], (BATCH, CTX_LEN, D), 1.0),
        'c_ctx': nrm(ks[3], (D,), 1.0),
        'ada_w': nrm(ks[4], (L, D, N_MOD * D), 0.5 * D ** -0.5),
        'ada_b': nrm(ks[5], (L, N_MOD * D), 0.01),
        'norm_mix_pre': gain(ks[6], (L, D)),
        'norm_mix_post': gain(ks[7], (L, D)),
        'norm_ffn_pre': gain(ks[8], (L, D)),
        'norm_ffn_post': gain(ks[9], (L, D)),
        'w_in': nrm(ks[10], (L, D, N_IN), D ** -0.5),
        'w_out': nrm(ks[11], (L, MIX_W, D), MIX_W ** -0.5),
        'q_norm': gain(ks[12], (L, HEAD_DIM)),
        'k_norm': gain(ks[13], (L, HEAD_DIM)),
        'rwkv_mu': jax.random.uniform(ks[14], (L, 2, RWKV_IN), f32, 0.0, 0.5),
        'rwkv_w0': nrm(ks[15], (L, 2, RWKV_W), 1.0),
        'rwkv_w_up': nrm(ks[16], (L, 2, DECAY_RANK, RWKV_W), 0.5 * DECAY_RANK ** -0.5),
        'rwkv_a0': nrm(ks[17], (L, 2, RWKV_W), 0.5),
        'rwkv_a_up': nrm(ks[18], (L, 2, ICLR_RANK, RWKV_W), 0.5 * ICLR_RANK ** -0.5),
        'rwkv_g_up': nrm(ks[19], (L, GATE_RANK, RWKV_W), GATE_RANK ** -0.5),
        'rwkv_k_k': 0.85 + nrm(ks[20], (L, RWKV_W), 0.05),
        'rwkv_k_a': gain(ks[21], (L, RWKV_W)),
        'rwkv_r_k': nrm(ks[22], (L, RWKV_HEADS, HEAD_DIM), 0.1),
        'rwkv_lnx_w': gain(ks[23], (L, RWKV_W)),
        'rwkv_lnx_b': nrm(ks[24], (L, RWKV_W), 0.01),
        'sconv_w': nrm(ks[25], (L, 3, CONV_W), 3 ** -0.5),
        'ffn_up': nrm(ks[26], (L, D, 2 * D_FF), D ** -0.5),
        'ffn_conv': nrm(ks[27], (L, 3, 2 * D_FF), 3 ** -0.5),
        'ffn_down': nrm(ks[28], (L, D_FF, D), D_FF ** -0.5),
    }


def reference(x, c, ctx, c_ctx, ada_w, ada_b, norm_mix_pre, norm_mix_post, norm_ffn_pre, norm_ffn_post,
              w_in, w_out, q_norm, k_norm, rwkv_mu, rwkv_w0, rwkv_w_up, rwkv_a0, rwkv_a_up, rwkv_g_up,
              rwkv_k_k, rwkv_k_a, rwkv_r_k, rwkv_lnx_w, rwkv_lnx_b, sconv_w, ffn_up, ffn_conv, ffn_down):
    B, T, _ = x.shape
    cos, sin = axial_angles(T)
    h_ctx = ctx
    s_zero = jnp.zeros((2, B, RWKV_HEADS, HEAD_DIM, HEAD_DIM), jnp.float32)
    for l in range(DEPTH):
        last = l == DEPTH - 1
        m_lat = jnp.split(jax.nn.silu(c) @ ada_w[l] + ada_b[l], N_MOD, axis=-1)
        m_ctx = jnp.split(jax.nn.silu(c_ctx)[None, :] @ ada_w[l] + ada_b[l], N_MOD, axis=-1)

        u_lat = modulate(rmsnorm(x, norm_mix_pre[l]), m_lat[0], m_lat[1]) @ w_in[l]
        u_ctx = modulate(rmsnorm(h_ctx, norm_mix_pre[l]), m_ctx[0], m_ctx[1]) @ w_in[l]
        at_l, rw_l, cv_l = jnp.split(u_lat, [ATTN_IN, ATTN_IN + RWKV_IN], axis=-1)
        at_c, rw_c, cv_c = jnp.split(u_ctx, [ATTN_IN, ATTN_IN + RWKV_IN], axis=-1)

        q_l, k_l, v_l = attn_heads(at_l, q_norm[l], k_norm[l])
        q_c, k_c, v_c = attn_heads(at_c, q_norm[l], k_norm[l])
        q_l = rope_2d(q_l, cos, sin)
        k_l = rope_2d(k_l, cos, sin)
        att_l = gqa_blocked(q_l, jnp.concatenate([k_c, k_l], axis=1), jnp.concatenate([v_c, v_l], axis=1))

        rw_p = (rwkv_mu[l], rwkv_w0[l], rwkv_w_up[l], rwkv_a0[l], rwkv_a_up[l], rwkv_g_up[l], rwkv_k_k[l], rwkv_k_a[l])
        t_c = rwkv_inputs(rw_c, *rw_p)
        t_l = rwkv_inputs(rw_l, *rw_p)
        s_ctx_final, y_c = rwkv_scan(s_zero, *scan_pack(*t_c[:6]))
        _, y_l = rwkv_scan(s_ctx_final, *scan_pack(*t_l[:6]))
        out_p = (rwkv_r_k[l], rwkv_lnx_w[l], rwkv_lnx_b[l])
        rwo_l = rwkv_output(dir_merge(y_l), t_l[0], t_l[2], t_l[3], t_l[6], *out_p).astype(x.dtype)

        sco_l = short_conv(cv_l, sconv_w[l])

        mix_l = jnp.concatenate([att_l, rwo_l, sco_l], axis=-1) @ w_out[l]
        x = x + m_lat[2][:, None, :] * rmsnorm(mix_l, norm_mix_post[l])
        f_l = conv_ffn(modulate(rmsnorm(x, norm_ffn_pre[l]), m_lat[3], m_lat[4]), ffn_up[l], ffn_conv[l], ffn_down[l])
        x = x + m_lat[5][:, None, :] * rmsnorm(f_l, norm_ffn_post[l])

        if not last:
            att_c = gqa(q_c, k_c, v_c)
            rwo_c = rwkv_output(dir_merge(y_c), t_c[0], t_c[2], t_c[3], t_c[6], *out_p).astype(h_ctx.dtype)
            sco_c = short_conv(cv_c, sconv_w[l])
            mix_c = jnp.concatenate([att_c, rwo_c, sco_c], axis=-1) @ w_out[l]
            h_ctx = h_ctx + m_ctx[2][:, None, :] * rmsnorm(mix_c, norm_mix_post[l])
            f_c = conv_ffn(modulate(rmsnorm(h_ctx, norm_ffn_pre[l]), m_ctx[3], m_ctx[4]), ffn_up[l], ffn_conv[l], ffn_down[l])
            h_ctx = h_ctx + m_ctx[5][:, None, :] * rmsnorm(f_c, norm_ffn_post[l])
    return x
```

```python
import contextlib
import numpy as np
import concourse.bass as bass
import concourse.mybir as mybir
from concourse.bass_utils import run_bass_kernel_spmd

F32 = mybir.dt.float32
BF16 = mybir.dt.bfloat16
AF = mybir.ActivationFunctionType
ALU = mybir.AluOpType
AX = mybir.AxisListType

ENGS = ("pe", "dve", "act", "pool", "sp")
NDMASEM = 12

D = 1024
BATCH = 4
SEQ = 4096
DEPTH = 4
CTX = 256
TT = CTX + SEQ
N_IN = 2688
D_FF = 2816
NORM_EPS = 1e-6
GN_EPS = 64e-5
DECAY_SCALE = 0.606531


class Buf:
    __slots__ = ("name", "w", "r")

    def __init__(self, name=""):
        self.name = name
        self.w = {}
        self.r = {}


class _Op:
    __slots__ = ("fn", "waits", "signal", "dma")

    def __init__(self, fn, waits, dma=None):
        self.fn = fn
        self.waits = waits
        self.signal = False
        self.dma = dma


class Prog:
    def __init__(self, nc):
        self.nc = nc
        self.ops = {e: [] for e in ENGS}
        self.snaps = {e: [] for e in ENGS}
        self.seen = {e: {} for e in ENGS}
        self.dma_n = {}
        self.dma_rr = {e: 0 for e in ENGS}

    def _need(self, eng, deps, raw_tokens):
        waits = {}
        for tok in deps:
            if tok is None:
                continue
            k, i = tok
            if k == eng and (tok not in raw_tokens or eng == "pe"):
                continue
            if self.seen[eng].get(k, -1) >= i:
                continue
            if waits.get(k, -1) < i:
                waits[k] = i
        for k, i in waits.items():
            self.seen[eng][k] = max(self.seen[eng].get(k, -1), i)
            if k in ENGS:
                self.ops[k][i].signal = True
                for kk, ii in self.snaps[k][i].items():
                    if self.seen[eng].get(kk, -1) < ii:
                        self.seen[eng][kk] = ii
        return list(waits.items())

    @staticmethod
    def _deps(reads, writes):
        deps = set()
        raw = set()
        for b in reads:
            for t in b.w.items():
                deps.add(t)
                raw.add(t)
        for b in writes:
            deps.update(b.w.items())
            deps.update(b.r.items())
        return deps, raw

    @staticmethod
    def _mark(tok, reads, writes):
        k, i = tok
        for b in reads:
            if b.r.get(k, -1) < i:
                b.r[k] = i
        for b in writes:
            if b.w.get(k, -1) < i:
                b.w[k] = i

    def op(self, eng, fn, reads=(), writes=()):
        deps, raw = self._deps(reads, writes)
        waits = self._need(eng, deps, raw)
        idx = len(self.ops[eng])
        self.ops[eng].append(_Op(fn, waits))
        self.snaps[eng].append(dict(self.seen[eng]))
        tok = (eng, idx)
        self._mark(tok, reads, writes)
        return tok

    def dma(self, q, fn, reads=(), writes=()):
        j = self.dma_rr[q] % NDMASEM
        self.dma_rr[q] += 1
        key = ("dma", q, j)
        n = self.dma_n.get(key, 0)
        deps, raw = self._deps(reads, writes)
        if n > 0:
            deps.add((key, 16 * n))
            raw.add((key, 16 * n))
        waits = self._need(q, deps, raw)
        self.ops[q].append(_Op(fn, waits, dma=key))
        self.snaps[q].append(dict(self.seen[q]))
        self.dma_n[key] = n + 1
        tok = (key, 16 * (n + 1))
        self._mark(tok, reads, writes)
        return tok

    def wait_all(self, eng, toks):
        waits = self._need(eng, set(toks), set(toks))
        self.ops[eng].append(_Op(None, waits))
        self.snaps[eng].append(dict(self.seen[eng]))

    def barrier(self):
        toks = []
        for e in ENGS:
            for i in range(len(self.ops[e]) - 1, -1, -1):
                o = self.ops[e][i]
                if o.fn is not None and o.dma is None:
                    toks.append((e, i))
                    break
        for key, n in self.dma_n.items():
            toks.append((key, 16 * n))
        for e in ENGS:
            self.wait_all(e, toks)

    def emit(self):
        nc = self.nc
        val = {}
        for e in ENGS:
            c = 0
            v = []
            for o in self.ops[e]:
                if o.signal and o.dma is None and o.fn is not None:
                    c += 1
                v.append(c)
            val[e] = v
        with contextlib.ExitStack() as st:
            sems = {}
            for e in ENGS:
                sems[e] = st.enter_context(nc.semaphore("s_" + e))
            for key in self.dma_n:
                sems[key] = st.enter_context(nc.semaphore("d_%s_%d" % (key[1], key[2])))
            block = st.enter_context(nc.Block())
            handles = {"pe": block.tensor, "dve": block.vector, "act": block.scalar,
                       "pool": block.gpsimd, "sp": block.sync}

            def mk(e):
                def body(eng):
                    for o in self.ops[e]:
                        for k, ii in o.waits:
                            eng.wait_ge(sems[k], val[k][ii] if k in ENGS else ii)
                        if o.fn is None:
                            continue
                        inst = o.fn(eng)
                        if o.dma is not None:
                            inst.then_inc(sems[o.dma], 16)
                        elif o.signal:
                            inst.then_inc(sems[e], 1)
                return body

            for e in ENGS:
                if self.ops[e]:
                    handles[e](mk(e))
        return nc


class Rot:
    def __init__(self, items):
        self.items = items
        self.i = 0

    def next(self):
        it = self.items[self.i % len(self.items)]
        self.i += 1
        return it


class KB:
    def __init__(self):
        self.nc = bass.Bass("TRN2", target_bir_lowering=False)
        self.st = contextlib.ExitStack()
        self.P = Prog(self.nc)
        self.outs = []
        self._n = 0

    def din(self, name, shape, dt=F32):
        return self.nc.dram_tensor(name, list(shape), dt, kind="ExternalInput").ap()

    def dout(self, name, shape, dt=F32):
        return self.nc.dram_tensor(name, list(shape), dt, kind="ExternalOutput").ap()

    def sb(self, shape, dt=F32, name=None):
        self._n += 1
        t = self.st.enter_context(self.nc.sbuf_tensor(name or "t%d" % self._n, list(shape), dt))
        return t, Buf(name or "t%d" % self._n)

    def ps(self, shape=(128, 512), dt=F32, name=None):
        self._n += 1
        t = self.st.enter_context(self.nc.psum_tensor(name or "p%d" % self._n, list(shape), dt))
        return t, Buf(name or "p%d" % self._n)

    def rot_sb(self, n, shape, dt=F32):
        return Rot([self.sb(shape, dt) for _ in range(n)])

    def rot_ps(self, n, shape=(128, 512), dt=F32):
        return Rot([self.ps(shape, dt) for _ in range(n)])

    def load(self, dst, src, b, q="sp"):
        return self.P.dma(q, lambda e: e.dma_start(out=dst, in_=src), writes=[b])

    def store(self, dst, src, b, q="sp"):
        tok = self.P.dma(q, lambda e: e.dma_start(out=dst, in_=src), reads=[b])
        self.outs.append(tok)
        return tok

    def finish(self):
        self.P.wait_all("sp", self.outs)
        self.P.emit()
        self.st.close()
        return self.nc


def run(nc, in_maps):
    res = run_bass_kernel_spmd(nc, in_maps, core_ids=list(range(8)))
    return res.results


_CACHE = {}


def cached(name, fn):
    if name not in _CACHE:
        _CACHE[name] = fn()
    return _CACHE[name]


def fm(v, nchunk):
    return np.ascontiguousarray(np.asarray(v, np.float32).reshape(nchunk, 128).T)


VEC = {}
_o = 0
for _name, _n in [("mu", 18), ("w0", 4), ("a0", 4), ("k_k", 2), ("k_a", 2), ("r_k", 2), ("lnx_w", 2), ("lnx_b", 2),
                  ("g_mix_pre", 8), ("g_mix_post", 8), ("g_ffn_pre", 8), ("g_ffn_post", 8), ("qn", 1), ("kn", 1),
                  ("sconv", 6), ("fconv", 132), ("ada_b", 48)]:
    VEC[_name] = _o
    _o += _n
NV = _o

CST = {}
_o = 0
for _name, _n in [("ident", 128), ("blk1", 128), ("mask4", 512), ("mL", 128), ("mask01", 512), ("ident2", 64),
                  ("prot", 128)]:
    CST[_name] = _o
    _o += _n
NCST = _o


def make_cst_np():
    c = np.zeros((128, NCST), np.float32)
    I = np.eye(128, dtype=np.float32)
    c[:, CST["ident"]:CST["ident"] + 128] = I
    blk = np.zeros((128, 128), np.float32)
    blk[0:64, 0:64] = 1
    blk[64:, 64:] = 1
    c[:, CST["blk1"]:CST["blk1"] + 128] = blk
    s = np.arange(128)[:, None]
    t = np.arange(128)[None, :]
    mUs = blk * (s < t)
    mUi = blk * (s <= t)
    c[:, CST["mask4"]:CST["mask4"] + 512] = np.concatenate([mUs, mUi, mUs, mUi], 1)
    c[:, CST["mL"]:CST["mL"] + 128] = blk * (s > t)
    m01 = np.ones((128, 512), np.float32)
    m01[:, ::64] = 0
    c[:, CST["mask01"]:CST["mask01"] + 512] = m01
    c[:, CST["ident2"]:CST["ident2"] + 64] = np.concatenate([np.eye(64), np.eye(64)], 0)
    pr = np.zeros((128, 128), np.float32)
    for h in range(2):
        for i in range(64):
            if i < 32:
                pr[h * 64 + i + 32, h * 64 + i] = -1.0
            else:
                pr[h * 64 + i - 32, h * 64 + i] = 1.0
    c[:, CST["prot"]:CST["prot"] + 128] = pr
    return c


SEGS = [(0, 256, 0, 256)] + [(256 + 512 * i, 512, 256, TT) for i in range(8)]


class G:
    pass


def stage_k3(kb, g, l):
    P = kb.P
    nc = kb.nc
    cst, bcst = g.cst, g.bcst
    vec, bvec = g.vec, g.bvec

    def C(name, n, rows=slice(0, 128)):
        return cst[rows, CST[name]:CST[name] + n]

    def Vv(name, i=0, n=1):
        o = VEC[name] + i
        return vec[:, l, o:o + n]

    main_st = kb.st
    dscr = {}

    def scratch(name, shape):
        if name not in g.scr:
            g.scr[name] = (nc.dram_tensor(name, list(shape), F32, kind="Internal").ap(), Buf(name))
        return g.scr[name]

    R_, bR = scratch("R", [256, TT])
    V_, bV = scratch("V", [256, TT])
    KK_, bKK = scratch("KK", [256, TT])
    G_, bG = scratch("G", [256, TT])
    BON_, bBON = scratch("BON", [256, TT])
    KE_ = [scratch("KE%d" % d, [256, TT]) for d in range(2)]
    KKA_ = [scratch("KKA%d" % d, [256, TT]) for d in range(2)]
    LW_ = [scratch("LW%d" % d, [256, TT]) for d in range(2)]
    YD_ = [scratch("YD%d" % d, [256, TT]) for d in range(2)]

    kb.st = contextlib.ExitStack()
    wup, bwup = kb.sb([128, 256])
    aup, baup = kb.sb([128, 256])
    gup, bgup = kb.sb([128, 256])
    kb.load(wup[:], g.w_up[l], bwup)
    kb.load(aup[:], g.a_up[l], baup)
    kb.load(gup[:], g.g_up[l], bgup)
    c0, bc0 = kb.sb([128, 9])
    muv = Vv("mu", 0, 18).rearrange("p (c i) -> p c i", i=2)
    P.op("dve", lambda e: e.tensor_tensor(out=c0[:], in0=muv[:, :, 0], in1=muv[:, :, 1], op=ALU.add), reads=[bvec], writes=[bc0])
    P.op("dve", lambda e: e.tensor_scalar(out=c0[:], in0=c0[:], scalar1=-1.0, scalar2=1.0, op0=ALU.mult, op1=ALU.add),
         reads=[bc0], writes=[bc0])
    tiny, btiny = kb.sb([128, 1])
    P.op("pool", lambda e: e.memset(tiny[:], 0.0), writes=[btiny])
    zinr = kb.rot_sb(2, [128, 9, 514])
    zsr = Rot([(kb.sb([128, 9, 512])[0], [Buf() for _ in range(9)]) for _ in range(2)])
    t512 = kb.rot_sb(24, [128, 512])
    atr = kb.rot_sb(2, [128, 2, 2, 512])
    kkr_ = kb.rot_sb(2, [128, 2, 512])
    pmm = kb.rot_ps(4)
    pbr = kb.rot_ps(2)
    UTrw = g.UT[768:1920, :].rearrange("(c p) t -> p c t", p=128)
    for (t0, n, slo, shi) in SEGS:
        zin, bzin = zinr.next()
        P.op("pool", lambda e, zin=zin: e.memset(zin[:, :, 0:1], 0.0), writes=[bzin])
        P.op("pool", lambda e, zin=zin, n=n: e.memset(zin[:, :, n + 1:n + 2], 0.0), writes=[bzin])
        lo, hi = max(t0 - 1, slo), min(t0 + n + 1, shi)
        kb.P.dma("sp", lambda e, zin=zin, lo=lo, hi=hi, t0=t0: e.dma_start(
            out=zin[:, :, lo - (t0 - 1):hi - (t0 - 1)], in_=UTrw[:, :, lo:hi]), reads=[g.bUT], writes=[bzin])
        zs, bzs = zsr.next()
        for c in range(9):
            P.op("pool", lambda e, c=c, zs=zs, zin=zin, n=n: e.tensor_scalar(
                out=zs[:, c, 0:n], in0=zin[:, c, 1:n + 1], scalar1=c0[:, c:c + 1], scalar2=0.0, op0=ALU.mult, op1=ALU.add),
                reads=[bzin, bc0], writes=[bzs[c]])
        for i, off in ((0, 0), (1, 2)):
            for c in range(9):
                P.op("dve", lambda e, c=c, zs=zs, zin=zin, n=n, i=i, off=off: e.scalar_tensor_tensor(
                    out=zs[:, c, 0:n], in0=zin[:, c, off:off + n], scalar=muv[:, c, i:i + 1], in1=zs[:, c, 0:n],
                    op0=ALU.mult, op1=ALU.add), reads=[bzin, bvec, bzs[c]], writes=[bzs[c]])
        P.dma("sp", lambda e, zs=zs, t0=t0, n=n: e.dma_start(
            out=R_[:, t0:t0 + n].rearrange("(c p) t -> p c t", p=128), in_=zs[:, 0:2, 0:n]), reads=[bzs[0], bzs[1]], writes=[bR])
        P.dma("sp", lambda e, zs=zs, t0=t0, n=n: e.dma_start(
            out=V_[:, t0:t0 + n].rearrange("(c p) t -> p c t", p=128), in_=zs[:, 4:6, 0:n]), reads=[bzs[4], bzs[5]], writes=[bV])
        tw, btw = t512.next()
        P.op("act", lambda e, tw=tw, zs=zs, n=n: e.activation(out=tw[:, 0:n], in_=zs[:, 6, 0:n], func=AF.Tanh), reads=[bzs[6]], writes=[btw])
        at, bat = atr.next()
        for d in range(2):
            ds_ = slice(d * 64, d * 64 + 64)
            for cc in range(2):
                cs_ = slice(cc * 128, cc * 128 + 128)
                pw, bpw = pmm.next()
                P.op("pe", lambda e, pw=pw, tw=tw, ds_=ds_, cs_=cs_, n=n: e.matmul(
                    pw[:, 0:n], lhsT=wup[ds_, cs_], rhs=tw[ds_, 0:n], start=True, stop=True), reads=[bwup, btw], writes=[bpw])
                lwt, blwt = t512.next()
                P.op("act", lambda e, pw=pw, lwt=lwt, d=d, cc=cc, n=n: e.activation(
                    out=lwt[:, 0:n], in_=pw[:, 0:n], func=AF.Sigmoid, bias=Vv("w0", d * 2 + cc), scale=1.0),
                    reads=[bpw, bvec], writes=[blwt])
                P.op("dve", lambda e, lwt=lwt, n=n: e.tensor_scalar(
                    out=lwt[:, 0:n], in0=lwt[:, 0:n], scalar1=-DECAY_SCALE, scalar2=None, op0=ALU.mult), reads=[blwt], writes=[blwt])
                P.dma("sp", lambda e, lwt=lwt, d=d, cs_=cs_, t0=t0, n=n: e.dma_start(
                    out=LW_[d][0][cs_, t0:t0 + n], in_=lwt[:, 0:n]), reads=[blwt], writes=[LW_[d][1]])
                pa, bpa = pmm.next()
                P.op("pe", lambda e, pa=pa, zs=zs, ds_=ds_, cs_=cs_, n=n: e.matmul(
                    pa[:, 0:n], lhsT=aup[ds_, cs_], rhs=zs[ds_, 7, 0:n], start=True, stop=True), reads=[baup, bzs[7]], writes=[bpa])
                P.op("act", lambda e, pa=pa, at=at, d=d, cc=cc, n=n: e.activation(
                    out=at[:, d, cc, 0:n], in_=pa[:, 0:n], func=AF.Sigmoid, bias=Vv("a0", d * 2 + cc), scale=1.0),
                    reads=[bpa, bvec], writes=[bat])
        sg, bsg = t512.next()
        P.op("act", lambda e, sg=sg, zs=zs, n=n: e.activation(out=sg[:, 0:n], in_=zs[:, 8, 0:n], func=AF.Sigmoid), reads=[bzs[8]], writes=[bsg])
        for cc in range(2):
            cs_ = slice(cc * 128, cc * 128 + 128)
            pg, bpg = pmm.next()
            P.op("pe", lambda e, pg=pg, sg=sg, cs_=cs_, n=n: e.matmul(pg[:, 0:n], lhsT=gup[:, cs_], rhs=sg[:, 0:n], start=True, stop=True),
                 reads=[bgup, bsg], writes=[bpg])
            gt, bgt = t512.next()
            P.op("act", lambda e, pg=pg, gt=gt, n=n: e.activation(out=gt[:, 0:n], in_=pg[:, 0:n], func=AF.Copy), reads=[bpg], writes=[bgt])
            P.dma("sp", lambda e, gt=gt, cs_=cs_, t0=t0, n=n: e.dma_start(out=G_[cs_, t0:t0 + n], in_=gt[:, 0:n]), reads=[bgt], writes=[bG])
        kkt, bkkt = kkr_.next()
        for cc in range(2):
            kr, bkr = t512.next()
            P.op("dve", lambda e, kr=kr, zs=zs, cc=cc, n=n: e.tensor_scalar(
                out=kr[:, 0:n], in0=zs[:, 2 + cc, 0:n], scalar1=Vv("k_k", cc), scalar2=None, op0=ALU.mult), reads=[bzs[2 + cc], bvec], writes=[bkr])
            sq, bsq = t512.next()
            P.op("pool", lambda e, kr=kr, sq=sq, n=n: e.tensor_tensor(out=sq[:, 0:n], in0=kr[:, 0:n], in1=kr[:, 0:n], op=ALU.mult), reads=[bkr], writes=[bsq])
            pss, bpss = pmm.next()
            P.op("pe", lambda e, pss=pss, sq=sq, n=n: e.matmul(pss[:, 0:n], lhsT=C("blk1", 128), rhs=sq[:, 0:n], start=True, stop=True),
                 reads=[bcst, bsq], writes=[bpss])
            rn, brn = t512.next()
            P.op("dve", lambda e, rn=rn, pss=pss, n=n: e.tensor_scalar(out=rn[:, 0:n], in0=pss[:, 0:n], scalar1=1e-12, scalar2=None, op0=ALU.max),
                 reads=[bpss], writes=[brn])
            P.op("act", lambda e, rn=rn, n=n: e.activation(out=rn[:, 0:n], in_=rn[:, 0:n], func=AF.Sqrt), reads=[brn], writes=[brn])
            P.op("dve", lambda e, rn=rn, n=n: e.reciprocal(out=rn[:, 0:n], in_=rn[:, 0:n]), reads=[brn], writes=[brn])
            P.op("dve", lambda e, rn=rn, kr=kr, kkt=kkt, cc=cc, n=n: e.tensor_tensor(out=kkt[:, cc, 0:n], in0=kr[:, 0:n], in1=rn[:, 0:n], op=ALU.mult),
                 reads=[brn, bkr], writes=[bkkt])
        P.dma("sp", lambda e, kkt=kkt, t0=t0, n=n: e.dma_start(
            out=KK_[:, t0:t0 + n].rearrange("(c p) t -> p c t", p=128), in_=kkt[:, :, 0:n]), reads=[bkkt], writes=[bKK])
        for cc in range(2):
            cs_ = slice(cc * 128, cc * 128 + 128)
            pb, bpb = pbr.next()
            for d in range(2):
                t1, bt1 = t512.next()
                P.op("dve", lambda e, t1=t1, at=at, d=d, cc=cc, n=n: e.tensor_scalar(
                    out=t1[:, 0:n], in0=at[:, d, cc, 0:n], scalar1=-1.0, scalar2=Vv("k_a", cc), op0=ALU.add, op1=ALU.mult),
                    reads=[bat, bvec], writes=[bt1])
                ke, bke = t512.next()
                P.op("dve", lambda e, t1=t1, ke=ke, zs=zs, cc=cc, n=n: e.scalar_tensor_tensor(
                    out=ke[:, 0:n], in0=t1[:, 0:n], scalar=1.0, in1=zs[:, 2 + cc, 0:n], op0=ALU.add, op1=ALU.mult),
                    reads=[bt1, bzs[2 + cc]], writes=[bke])
                P.dma("sp", lambda e, ke=ke, d=d, cs_=cs_, t0=t0, n=n: e.dma_start(out=KE_[d][0][cs_, t0:t0 + n], in_=ke[:, 0:n]),
                      reads=[bke], writes=[KE_[d][1]])
                ka, bka = t512.next()
                P.op("pool", lambda e, ka=ka, kkt=kkt, at=at, d=d, cc=cc, n=n: e.tensor_tensor(
                    out=ka[:, 0:n], in0=kkt[:, cc, 0:n], in1=at[:, d, cc, 0:n], op=ALU.mult), reads=[bkkt, bat], writes=[bka])
                P.dma("sp", lambda e, ka=ka, d=d, cs_=cs_, t0=t0, n=n: e.dma_start(out=KKA_[d][0][cs_, t0:t0 + n], in_=ka[:, 0:n]),
                      reads=[bka], writes=[KKA_[d][1]])
                pr, bpr = t512.next()
                P.op("dve", lambda e, pr=pr, ke=ke, zs=zs, cc=cc, n=n: e.scalar_tensor_tensor(
                    out=pr[:, 0:n], in0=zs[:, cc, 0:n], scalar=Vv("r_k", cc), in1=ke[:, 0:n], op0=ALU.mult, op1=ALU.mult),
                    reads=[bke, bzs[cc], bvec], writes=[bpr])
                P.op("pe", lambda e, pb=pb, pr=pr, d=d, n=n: e.matmul(pb[:, 0:n], lhsT=C("blk1", 128), rhs=pr[:, 0:n], start=(d == 0), stop=(d == 1)),
                     reads=[bcst, bpr], writes=[bpb])
            bo, bbo = t512.next()
            P.op("dve", lambda e, bo=bo, pb=pb, zs=zs, cc=cc, n=n: e.tensor_tensor(out=bo[:, 0:n], in0=pb[:, 0:n], in1=zs[:, 4 + cc, 0:n], op=ALU.mult),
                 reads=[bpb, bzs[4 + cc]], writes=[bbo])
            P.dma("sp", lambda e, bo=bo, cs_=cs_, t0=t0, n=n: e.dma_start(out=BON_[cs_, t0:t0 + n], in_=bo[:, 0:n]), reads=[bbo], writes=[bBON])
    P.barrier()
    kb.st.close()

    kb.st = contextlib.ExitStack()
    srcr = [kb.rot_sb(1, [128, 512]) for _ in range(6)]
    der = kb.rot_sb(8, [128, 512])
    hatr = [kb.rot_sb(1, [128, 512], BF16 if i_ in (2, 3) else F32) for i_ in range(7)]
    hatb = [kb.rot_sb(1, [128, 512], BF16) for _ in range(2)]
    wcr = kb.rot_sb(1, [128, 8])
    ynr = kb.rot_sb(1, [128, 512])
    A4r = kb.rot_sb(4, [128, 512], BF16)
    TOKr = kb.rot_sb(4, [128, 4, 128], BF16)
    sqset = [[[kb.sb([128, 2, 128], BF16) for _ in range(3)] for _ in range(6)] for _ in range(2)]
    MTr = kb.rot_sb(4, [128, 2, 128])
    Gsr = kb.rot_sb(4, [128, 2, 128])
    R2r = kb.rot_sb(4, [128, 128])
    Y0r = kb.rot_sb(4, [128, 128])
    STr = kb.rot_sb(2, [128, 128])
    for (tl, bt) in MTr.items + Gsr.items + STr.items:
        P.op("pool", lambda e, tl=tl: e.memset(tl[:], 0.0), writes=[bt])
    identb, bidentb = kb.sb([128, 128], BF16)
    P.op("pool", lambda e: e.tensor_copy(out=identb[:], in_=cst[:, CST["ident"]:CST["ident"] + 128]), reads=[bcst], writes=[bidentb])
    pool = kb.rot_ps(5)
    pMGs = [kb.ps() for _ in range(2)]
    pSQ = kb.ps()[0]
    bpSQ = Buf()
    pSQslots = Rot([(pSQ[:, i * 128:(i + 1) * 128], bpSQ) for i in range(4)])
    Zrh = [kb.rot_sb(3, [128, 2, 128], BF16) for _ in range(2)]
    m2rh = [kb.rot_sb(3, [128, 2, 64], BF16) for _ in range(4)]
    seti = 0
    ev = [0]

    def evac_copy(out, in_, reads, writes):
        ev[0] += 1
        if ev[0] % 2:
            P.op("act", lambda e: e.activation(out=out, in_=in_, func=AF.Copy), reads=reads, writes=writes)
        else:
            P.op("dve", lambda e: e.tensor_copy(out=out, in_=in_), reads=reads, writes=writes)

    for d in range(2):
        order = SEGS if d == 0 else [SEGS[0]] + SEGS[:0:-1]
        for hp in range(2):
            rows = slice(hp * 128, hp * 128 + 128)
            ST, bST = STr.next()
            P.op("pool", lambda e, ST=ST: e.memset(ST[:], 0.0), writes=[bST])
            stt_ = [(ST, bST)]
            for (t0, n, slo, shi) in order:
                nch = n // 64

                def rv(ap, n=n, d=d):
                    a = ap[:, 0:n]
                    return a[:, ::-1] if d == 1 else a

                srcs = []
                for i, (dr, bdr) in enumerate([(R_, bR), KE_[d], (V_, bV), (KK_, bKK), KKA_[d], LW_[d]]):
                    tl, btl = srcr[i].next()
                    P.dma("sp" if i % 2 == 0 else "act", lambda e, tl=tl, dr=dr, rows=rows, t0=t0, n=n: e.dma_start(out=tl[:, 0:n], in_=dr[rows, t0:t0 + n]),
                          reads=[bdr], writes=[btl])
                    srcs.append((tl, btl))
                (r_s, br_s), (ke_s, bke_s), (v_s, bv_s), (kk_s, bkk_s), (kka_s, bkka_s), (lw_s, blw_s) = srcs
                lwS, blwS = der.next()
                P.op("pool", lambda e, lwS=lwS, lw_s=lw_s, rv=rv, n=n: e.tensor_copy(out=lwS[:, 0:n], in_=rv(lw_s)), reads=[blw_s], writes=[blwS])
                L, bL = der.next()
                P.op("dve", lambda e, L=L, lwS=lwS, n=n: e.tensor_tensor_scan(
                    out=L[:, 0:n], data0=C("mask01", n), data1=lwS[:, 0:n], initial=0.0, op0=ALU.mult, op1=ALU.add),
                    reads=[blwS, bcst], writes=[bL])
                Lex, bLex = der.next()
                P.op("pool", lambda e, Lex=Lex, L=L, lwS=lwS, n=n: e.tensor_tensor(out=Lex[:, 0:n], in0=L[:, 0:n], in1=lwS[:, 0:n], op=ALU.subtract),
                     reads=[bL, blwS], writes=[bLex])
                Lc, bLc = der.next()
                L3 = L[:, 0:n].rearrange("p (c j) -> p c j", j=64)
                P.op("dve", lambda e, Lc=Lc, L3=L3, n=n, nch=nch: e.tensor_tensor(
                    out=Lc[:, 0:n].rearrange("p (c j) -> p c j", j=64), in0=L3[:, :, 63:64].broadcast_to([128, nch, 64]), in1=L3,
                    op=ALU.subtract), reads=[bL], writes=[bLc])
                eL, beL = der.next()
                eLm, beLm = der.next()
                eLex, beLex = der.next()
                eLc, beLc = der.next()
                WC, bWC = wcr.next()
                P.op("act", lambda e, eL=eL, L=L, n=n: e.activation(out=eL[:, 0:n], in_=L[:, 0:n], func=AF.Exp), reads=[bL], writes=[beL])
                P.op("act", lambda e, eLm=eLm, L=L, n=n: e.activation(out=eLm[:, 0:n], in_=L[:, 0:n], func=AF.Exp, scale=-1.0), reads=[bL], writes=[beLm])
                P.op("act", lambda e, eLex=eLex, Lex=Lex, n=n: e.activation(out=eLex[:, 0:n], in_=Lex[:, 0:n], func=AF.Exp), reads=[bLex], writes=[beLex])
                P.op("act", lambda e, eLc=eLc, Lc=Lc, n=n: e.activation(out=eLc[:, 0:n], in_=Lc[:, 0:n], func=AF.Exp), reads=[bLc], writes=[beLc])
                P.op("act", lambda e, WC=WC, L3=L3, nch=nch: e.activation(out=WC[:, 0:nch], in_=L3[:, :, 63], func=AF.Exp), reads=[bL], writes=[bWC])
                hats = [h_.next() for h_ in hatr]
                (AhT, bAh), (RhT, bRh), (BhT, bBh), (KhT, bKh), (BtT, bBt), (KtT, bKt), (vS, bvS) = hats
                P.op("dve", lambda e, AhT=AhT, kk_s=kk_s, eLex=eLex, rv=rv, n=n: e.scalar_tensor_tensor(
                    out=AhT[:, 0:n], in0=rv(kk_s), scalar=-1.0, in1=eLex[:, 0:n], op0=ALU.mult, op1=ALU.mult), reads=[bkk_s, beLex], writes=[bAh])
                P.op("pool", lambda e, RhT=RhT, r_s=r_s, eL=eL, rv=rv, n=n: e.tensor_tensor(out=RhT[:, 0:n], in0=rv(r_s), in1=eL[:, 0:n], op=ALU.mult),
                     reads=[br_s, beL], writes=[bRh])
                P.op("dve", lambda e, BhT=BhT, kka_s=kka_s, eLm=eLm, rv=rv, n=n: e.tensor_tensor(out=BhT[:, 0:n], in0=rv(kka_s), in1=eLm[:, 0:n], op=ALU.mult),
                     reads=[bkka_s, beLm], writes=[bBh])
                P.op("pool", lambda e, KhT=KhT, ke_s=ke_s, eLm=eLm, rv=rv, n=n: e.tensor_tensor(out=KhT[:, 0:n], in0=rv(ke_s), in1=eLm[:, 0:n], op=ALU.mult),
                     reads=[bke_s, beLm], writes=[bKh])
                P.op("dve", lambda e, BtT=BtT, kka_s=kka_s, eLc=eLc, rv=rv, n=n: e.tensor_tensor(out=BtT[:, 0:n], in0=rv(kka_s), in1=eLc[:, 0:n], op=ALU.mult),
                     reads=[bkka_s, beLc], writes=[bBt])
                P.op("pool", lambda e, KtT=KtT, ke_s=ke_s, eLc=eLc, rv=rv, n=n: e.tensor_tensor(out=KtT[:, 0:n], in0=rv(ke_s), in1=eLc[:, 0:n], op=ALU.mult),
                     reads=[bke_s, beLc], writes=[bKt])
                P.op("pool", lambda e, vS=vS, v_s=v_s, rv=rv, n=n: e.tensor_copy(out=vS[:, 0:n], in_=rv(v_s)), reads=[bv_s], writes=[bvS])
                (AhTb, bAhb), (RhTb, bRhb) = [h_.next() for h_ in hatb]
                CP_(P, "act", AhTb[:, 0:n], AhT[:, 0:n], [bAh], [bAhb])
                CP_(P, "act", RhTb[:, 0:n], RhT[:, 0:n], [bRh], [bRhb])
                Yn, bYn = ynr.next()
                Yv = Yn[:, 0:n][:, ::-1] if d == 1 else Yn[:, 0:n]
                blk3 = C("blk1", 128).rearrange("p (c j) -> p c j", j=64)
                pending = [[]]

                def cp_gen(gi, pMG, bpMG, tr, TOK, bTOK, AhT=AhTb, bAh=bAhb, RhT=RhTb, bRh=bRhb, BhT=BhT, bBh=bBh, KhT=KhT, bKh=bKh):
                    HS = [slice(0, 64), slice(64, 128)]
                    S = sqset[gi]
                    ident2h = C("ident", 128).unsqueeze(1).broadcast_to([128, 2, 128])
                    mL2h = C("mL", 128).unsqueeze(1).broadcast_to([128, 2, 128])
                    A4s = []
                    for h in range(2):
                        hs = HS[h]
                        pa, bpa = pool.next()
                        for qi, (lt, blt, rt, brt) in enumerate([(BhT, bBh, AhT, bAh), (BhT, bBh, RhT, bRh), (KhT, bKh, AhT, bAh), (KhT, bKh, RhT, bRh)]):
                            MM_(P, pa[:, qi * 128:(qi + 1) * 128], lt[hs, tr], rt[hs, tr], True, True, [blt, brt], [bpa])
                        A4, bA4 = A4r.next()
                        TT_(P, "dve", A4[:], pa[:], C("mask4", 512), ALU.mult, [bcst], [bA4, bpa])
                        A4s.append((A4, bA4))
                    (N0, bN0), (NT0, bNT0), (Ip0, bIp0) = S[0]
                    pxs = [pool.next(), pool.next()]
                    for h in range(2):
                        px, bpx = pxs[h]
                        MM_(P, px[:, 0:128], AhT[HS[h], tr], BhT[HS[h], tr], True, True, [bAh, bBh], [bpx])
                    for h in range(2):
                        px, bpx = pxs[h]
                        TT_(P, "dve", N0[:, h, :], px[:, 0:128], C("mL", 128), ALU.mult, [bcst], [bN0, bpx])
                    yield
                    pk, bpk = pxs[0]
                    for h in range(2):
                        MM_(P, pk[:, 128 + h * 64:192 + h * 64], A4s[h][0][:, 256:384], TOK[:, 3, HS[h]], True, True, [A4s[h][1], bTOK], [bpk])
                    Z, bZ = Zrh[gi].next()
                    CP_(P, "pool", Z[:, :, 0:64], TOK[:, 0, :].rearrange("p (h k) -> p h k", k=64), [bTOK], [bZ])
                    CP_(P, "dve", Z[:, :, 64:128], pk[:, 128:256].rearrange("p (h k) -> p h k", k=64), [], [bZ, bpk])
                    yield
                    for i in range(6):
                        (Ni, bNi), (NTi, bNTi), (Ipi, bIpi) = S[i]

                        def NT_ap(h, i=i, NTi=NTi):
                            return A4s[h][0][:, 0:128] if i == 0 else NTi[:, h, :]

                        def NT_b(h, i=i, bNTi=bNTi):
                            return A4s[h][1] if i == 0 else bNTi
                        L1, bL1 = pool.next()
                        for h in range(2):
                            MM_(P, L1[:, h * 128:(h + 1) * 128], NT_ap(h), Z[:, h, :], True, False, [NT_b(h), bZ], [bL1])
                            MM_(P, L1[:, h * 128:(h + 1) * 128], identb[:], Z[:, h, :], False, True, [bidentb, bZ], [bL1])
                        if i < 5:
                            (Nn, bNn), (NTn, bNTn), (Ipn, bIpn) = S[i + 1]
                            for h in range(2):
                                MM_(P, L1[:, 256 + h * 128:256 + (h + 1) * 128], Ni[:, h, :], NT_ap(h), True, True, [bNi, NT_b(h)], [bL1])
                        if i < 4:
                            L2, bL2 = pool.next()
                            for h in range(2):
                                MM_(P, L2[:, h * 128:(h + 1) * 128], NT_ap(h), Ni[:, h, :], True, True, [bNi, NT_b(h)], [bL2])
                        Z2, bZ2 = Zrh[gi].next()
                        evac_copy(Z2[:].rearrange("p h k -> p (h k)"), L1[:, 0:256], [], [bZ2, bL1])
                        Z, bZ = Z2, bZ2
                        if i < 5:
                            evac_copy(NTn[:].rearrange("p h k -> p (h k)"), L1[:, 256:512], [], [bNTn, bL1])
                        if i < 4:
                            CP_(P, "act", Nn[:].rearrange("p h k -> p (h k)"), L2[:, 0:256], [], [bNn, bL2])
                        yield
                    m2 = []
                    for h in range(2):
                        hs = HS[h]
                        B2, bB2 = m2rh[gi * 2 + h].next()
                        V2, bV2 = m2rh[gi * 2 + h].next()
                        Q2, bQ2 = m2rh[gi * 2 + h].next()
                        TT_(P, "pool", B2[:], TOK[:, 1, hs].unsqueeze(1).broadcast_to([128, 2, 64]), blk3, ALU.mult, [bTOK, bcst], [bB2])
                        TT_(P, "pool", V2[:], TOK[:, 3, hs].unsqueeze(1).broadcast_to([128, 2, 64]), blk3, ALU.mult, [bTOK, bcst], [bV2])
                        TT_(P, "dve", Q2[:], Z[:, h, 64:128].unsqueeze(1).broadcast_to([128, 2, 64]), blk3, ALU.mult, [bZ, bcst], [bQ2])
                        m2.append((B2, bB2, V2, bV2, Q2, bQ2))
                    yield
                    for h in range(2):
                        hs = HS[h]
                        B2, bB2, V2, bV2, Q2, bQ2 = m2[h]
                        A4, bA4 = A4s[h]
                        MM_(P, pMG[hs, 0:128], Z[:, h, 0:64], B2[:].rearrange("p c j -> p (c j)"), True, True, [bZ, bB2], [bpMG])
                        MM_(P, pMG[hs, 128:256], TOK[:, 1, hs], Q2[:].rearrange("p c j -> p (c j)"), True, False, [bTOK, bQ2], [bpMG])
                        MM_(P, pMG[hs, 128:256], TOK[:, 2, hs], V2[:].rearrange("p c j -> p (c j)"), False, True, [bTOK, bV2], [bpMG])
                        MM_(P, pMG[hs, 256:384], Z[:, h, 0:64], A4[:, 128:256], True, True, [bZ, bA4], [bpMG])
                        MM_(P, pMG[hs, 384:512], Z[:, h, 64:128], A4[:, 128:256], True, False, [bZ, bA4], [bpMG])
                        MM_(P, pMG[hs, 384:512], TOK[:, 3, hs], A4[:, 384:512], False, True, [bTOK, bA4], [bpMG])

                def make_seq(cp, MTs, bMTs, Gs, bGs, R2, bR2, Y0, bY0, Yv=Yv, bYn=bYn):
                    def mk(c):
                        def run():
                            ST, bST = stt_[0]
                            cs = slice(c * 64, c * 64 + 64)
                            ps_, bps_ = pSQslots.next()
                            MM_(P, ps_[:, 0:64], ST[:], R2[:, cs], True, True, [bST, bR2], [bps_])
                            c0_ = cp * 128 + c * 64
                            ps2, bps2 = pSQslots.next()
                            MM_(P, ps2, MTs[:, c, :], ST[:], True, True, [bST, bMTs], [bps2])
                            ST2, bST2 = STr.next()
                            TT_(P, "dve", ST2[:], ps2, Gs[:, c, :], ALU.add, [bGs], [bST2, bps2])
                            TT_(P, "dve", Yv[:, c0_:c0_ + 64], ps_[:, 0:64], Y0[:, cs], ALU.add, [bY0], [bYn, bps_])
                            stt_[0] = (ST2, bST2)
                        return run
                    return [mk(0), mk(1)]

                for cp0 in range(0, n // 128, 2):
                    gens = []
                    infos = []
                    for ci_, cp in enumerate((cp0, cp0 + 1)):
                        tr = slice(cp * 128, cp * 128 + 128)
                        ptr_, bptr_ = pool.next()
                        for wi, (src, bsrc) in enumerate([(AhT, bAh), (BtT, bBt), (KtT, bKt), (vS, bvS)]):
                            P.op("pe", lambda e, wi=wi, src=src, tr=tr, ptr_=ptr_: e.transpose(ptr_[:, wi * 128:(wi + 1) * 128], src[:, tr], C("ident", 128)),
                                 reads=[bsrc, bcst], writes=[bptr_])
                        TOK, bTOK = TOKr.next()
                        evac_copy(TOK[:].rearrange("p w k -> p (w k)"), ptr_[:], [], [bTOK, bptr_])
                        pMG, bpMG = pMGs[ci_]
                        infos.append((cp, tr, pMG, bpMG))
                        gens.append(cp_gen(ci_, pMG, bpMG, tr, TOK, bTOK))
                    rounds = 0
                    while gens:
                        for gn in list(gens):
                            try:
                                next(gn)
                            except StopIteration:
                                gens.remove(gn)
                        rounds += 1
                        if rounds >= 2 and pending[0]:
                            pending[0].pop(0)()
                    while pending[0]:
                        pending[0].pop(0)()
                    newp = []
                    for (cp, tr, pMG, bpMG) in infos:
                        MTs, bMTs = MTr.next()
                        Gs, bGs = Gsr.next()
                        for c in range(2):
                            ci = cp * 2 + c
                            for h in range(2):
                                hs = slice(h * 64, h * 64 + 64)
                                STT_(P, MTs[hs, c, hs], cst[hs, CST["ident"] + hs.start:CST["ident"] + hs.stop], WC[hs, ci:ci + 1],
                                     pMG[hs, c * 64:c * 64 + 64], ALU.mult, ALU.add, [bWC, bcst], [bMTs, bpMG])
                                CP_(P, "act", Gs[hs, c, hs], pMG[hs, 128 + c * 64:128 + c * 64 + 64], [], [bGs, bpMG])
                        R2, bR2 = R2r.next()
                        TT_(P, "dve", R2[:], pMG[:, 256:384], RhT[:, tr], ALU.add, [bRh], [bR2, bpMG])
                        Y0, bY0 = Y0r.next()
                        CP_(P, "act", Y0[:], pMG[:, 384:512], [], [bY0, bpMG])
                        newp.extend(make_seq(cp, MTs, bMTs, Gs, bGs, R2, bR2, Y0, bY0))
                    pending[0] = newp
                while pending[0]:
                    pending[0].pop(0)()
                P.dma("sp", lambda e, Yn=Yn, d=d, rows=rows, t0=t0, n=n: e.dma_start(out=YD_[d][0][rows, t0:t0 + n], in_=Yn[:, 0:n]),
                      reads=[bYn], writes=[YD_[d][1]])
    P.barrier()
    kb.st.close()

    kb.st = contextlib.ExitStack()
    o512 = kb.rot_sb(16, [128, 512])
    pO = kb.rot_ps(4)
    gne, bgne = kb.sb([128, 1])
    P.op("pool", lambda e: e.memset(gne[:], GN_EPS), writes=[bgne])
    for (t0, n, slo, shi) in SEGS:
        for cc in range(2):
            cs_ = slice(cc * 128, cc * 128 + 128)
            ya, bya = o512.next()
            yb, byb = o512.next()
            bo, bbo = o512.next()
            gg, bgg = o512.next()
            kb.load(ya[:, 0:n], YD_[0][0][cs_, t0:t0 + n], bya)
            kb.load(yb[:, 0:n], YD_[1][0][cs_, t0:t0 + n], byb, q="act")
            kb.load(bo[:, 0:n], BON_[cs_, t0:t0 + n], bbo)
            kb.load(gg[:, 0:n], G_[cs_, t0:t0 + n], bgg, q="act")
            P.op("pool", lambda e, ya=ya, yb=yb, n=n: e.tensor_tensor(out=ya[:, 0:n], in0=ya[:, 0:n], in1=yb[:, 0:n], op=ALU.add), reads=[bya, byb], writes=[bya])
            pm_, bpm_ = pO.next()
            P.op("pe", lambda e, pm_=pm_, ya=ya, n=n: e.matmul(pm_[:, 0:n], lhsT=C("blk1", 128), rhs=ya[:, 0:n], start=True, stop=True), reads=[bya, bcst], writes=[bpm_])
            yc, byc = o512.next()
            P.op("dve", lambda e, yc=yc, pm_=pm_, ya=ya, n=n: e.scalar_tensor_tensor(
                out=yc[:, 0:n], in0=pm_[:, 0:n], scalar=-1.0 / 64, in1=ya[:, 0:n], op0=ALU.mult, op1=ALU.add), reads=[bpm_, bya], writes=[byc])
            sq, bsq = o512.next()
            P.op("pool", lambda e, sq=sq, yc=yc, n=n: e.tensor_tensor(out=sq[:, 0:n], in0=yc[:, 0:n], in1=yc[:, 0:n], op=ALU.mult), reads=[byc], writes=[bsq])
            pv, bpv = pO.next()
            P.op("pe", lambda e, pv=pv, sq=sq, n=n: e.matmul(pv[:, 0:n], lhsT=C("blk1", 128), rhs=sq[:, 0:n], start=True, stop=True), reads=[bsq, bcst], writes=[bpv])
            rs, brs = o512.next()
            P.op("act", lambda e, rs=rs, pv=pv, n=n: e.activation(out=rs[:, 0:n], in_=pv[:, 0:n], func=AF.Sqrt, bias=gne[:], scale=1.0 / 64), reads=[bpv, bgne], writes=[brs])
            P.op("dve", lambda e, rs=rs, n=n: e.reciprocal(out=rs[:, 0:n], in_=rs[:, 0:n]), reads=[brs], writes=[brs])
            P.op("dve", lambda e, yc=yc, rs=rs, cc=cc, n=n: e.scalar_tensor_tensor(
                out=yc[:, 0:n], in0=yc[:, 0:n], scalar=Vv("lnx_w", cc), in1=rs[:, 0:n], op0=ALU.mult, op1=ALU.mult), reads=[byc, brs, bvec], writes=[byc])
            P.op("dve", lambda e, yc=yc, bo=bo, cc=cc, n=n: e.scalar_tensor_tensor(
                out=yc[:, 0:n], in0=yc[:, 0:n], scalar=Vv("lnx_b", cc), in1=bo[:, 0:n], op0=ALU.add, op1=ALU.add), reads=[byc, bbo, bvec], writes=[byc])
            P.op("pool", lambda e, yc=yc, gg=gg, n=n: e.tensor_tensor(out=yc[:, 0:n], in0=yc[:, 0:n], in1=gg[:, 0:n], op=ALU.mult), reads=[byc, bgg], writes=[byc])
            P.dma("sp", lambda e, yc=yc, cs_=cs_, t0=t0, n=n: e.dma_start(out=g.RWO[cs_, t0:t0 + n], in_=yc[:, 0:n]), reads=[byc], writes=[g.bRWO])
    P.barrier()
    kb.st.close()
    kb.st = main_st


def TT_(P, eng, out, a, b, op, r, w):
    return P.op(eng, lambda e: e.tensor_tensor(out=out, in0=a, in1=b, op=op), reads=r, writes=w)


def TS_(P, eng, out, a, s1, s2, op0, op1, r, w):
    if op1 is None:
        return P.op(eng, lambda e: e.tensor_scalar(out=out, in0=a, scalar1=s1, scalar2=None, op0=op0), reads=r, writes=w)
    return P.op(eng, lambda e: e.tensor_scalar(out=out, in0=a, scalar1=s1, scalar2=s2, op0=op0, op1=op1), reads=r, writes=w)


def STT_(P, out, a, s, b, op0, op1, r, w):
    return P.op("dve", lambda e: e.scalar_tensor_tensor(out=out, in0=a, scalar=s, in1=b, op0=op0, op1=op1), reads=r, writes=w)


def ACT_(P, out, in_, func, r, w, bias=None, scale=1.0):
    if bias is None:
        return P.op("act", lambda e: e.activation(out=out, in_=in_, func=func, scale=scale), reads=r, writes=w)
    return P.op("act", lambda e: e.activation(out=out, in_=in_, func=func, bias=bias, scale=scale), reads=r, writes=w)


def MM_(P, out, lhsT, rhs, start, stop, r, w):
    return P.op("pe", lambda e: e.matmul(out, lhsT=lhsT, rhs=rhs, start=start, stop=stop), reads=r, writes=w)


def CP_(P, eng, out, in_, r, w):
    if eng == "act":
        return P.op("act", lambda e: e.activation(out=out, in_=in_, func=AF.Copy), reads=r, writes=w)
    return P.op(eng, lambda e: e.tensor_copy(out=out, in_=in_), reads=r, writes=w)


def DMA_(P, q, out, in_, r, w):
    return P.dma(q, lambda e: e.dma_start(out=out, in_=in_), reads=r, writes=w)


def MS_(P, eng, ap, val, w):
    return P.op(eng, lambda e: e.memset(ap, val), writes=w)


def rstd8(kb, g, x, bx, n, sq, bsq, pss, bps, rstd, brs):
    P = kb.P
    TT_(P, "pool", sq, x, x, ALU.mult, [bx], [bsq])
    for c in range(8):
        MM_(P, pss, g.ones_bf[:], sq[:, c, :], c == 0, c == 7, [bsq, g.b_ones], [bps])
    ACT_(P, rstd, pss, AF.Sqrt, [g.b_eps], [brs, bps], bias=g.eps[:], scale=1.0 / D)
    P.op("dve", lambda e: e.reciprocal(out=rstd, in_=rstd), reads=[brs], writes=[brs])


def load_w_bf16(kb, dst, bdst_list, src_rows_fn, nrow_chunks, ncols, stg_rot):
    P = kb.P
    for kc in range(nrow_chunks):
        st_, bst = stg_rot.next()
        DMA_(P, "pool" if kc % 2 else "sp", st_[:, 0:ncols], src_rows_fn(kc), [], [bst])
        CP_(P, "act" if kc % 2 else "dve", dst[:, kc, :], st_[:, 0:ncols], [bst], [bdst_list[kc]])


def stage_ada(kb, g):
    P = kb.P
    main = kb.st
    kb.st = contextlib.ExitStack()
    ct, bc = kb.sb([128, 8, 2])
    sc, bsc = kb.sb([128, 8, 2])
    DMA_(P, "sp", ct[:].rearrange("p c r -> p (c r)"), g.cT[:, :], [], [bc])
    ACT_(P, sc[:], ct[:], AF.Silu, [bc], [bsc])
    wr = kb.rot_sb(3, [128, 8, 128])
    pr = kb.rot_ps(2, [128, 8])
    for l in range(g.nl):
        wv = g.ada_w[l].rearrange("(c p) n -> p c n", p=128)
        for j in range(48):
            wt, bw = wr.next()
            DMA_(P, "sp" if j % 2 == 0 else "pool", wt[:], wv[:, :, j * 128:(j + 1) * 128], [], [bw])
            pt, bp = pr.next()
            for kc in range(8):
                MM_(P, pt[:, 0:2], wt[:, kc, :], sc[:, kc, :], kc == 0, kc == 7, [bw, bsc], [bp])
            TS_(P, "dve", g.mt[:, l, j, :], pt[:, 0:2], g.vec[:, l, VEC["ada_b"] + j:VEC["ada_b"] + j + 1], None, ALU.add, None,
                [g.bvec], [g.bmt, bp])
    P.barrier()
    kb.st.close()
    kb.st = main


def mod_ap(g, l, i, c, s):
    return g.mt[:, l, i * 8 + c, s:s + 1]


def norm_mod_tile(kb, g, l, xt, bx, n, seg, gs, bgs, shift_i, xm, bxm, rot):
    P = kb.P
    sq, bsq = rot["sq"].next()
    ps_, bps = rot["pss"].next()
    rs, brs = rot["rs"].next()
    rstd8(kb, g, xt[:, :, 0:n], bx, n, sq[:, :, 0:n], bsq, ps_[:, 0:n], bps, rs[:, 0:n], brs)
    for c in range(8):
        tmp, btmp = rot["tmp"].next()
        STT_(P, tmp[:, 0:n], xt[:, c, 0:n], gs[:, seg, c:c + 1], rs[:, 0:n], ALU.mult, ALU.mult, [bx, bgs, brs], [btmp])
        ACT_(P, xm[:, c, 0:n], tmp[:, 0:n], AF.Identity, [btmp, g.bmt], [bxm], bias=mod_ap(g, l, shift_i, c, seg), scale=1.0)


def make_gs(kb, g, l, gname, scale_i):
    P = kb.P
    gs, bgs = kb.sb([128, 2, 8])
    for s in range(2):
        for c in range(8):
            STT_(P, gs[:, s, c:c + 1], mod_ap(g, l, scale_i, c, s), 1.0, g.vec[:, l, VEC[gname] + c:VEC[gname] + c + 1], ALU.add, ALU.mult,
                 [g.bmt, g.bvec], [bgs])
    return gs, bgs


def make_gp(kb, g, l, gname, gate_i):
    P = kb.P
    gp, bgp = kb.sb([128, 2, 8])
    for s in range(2):
        for c in range(8):
            TT_(P, "dve", gp[:, s, c:c + 1], mod_ap(g, l, gate_i, c, s), g.vec[:, l, VEC[gname] + c:VEC[gname] + c + 1], ALU.mult,
                [g.bmt, g.bvec], [bgp])
    return gp, bgp


def xsrc_tile(g, l, t0, n):
    if t0 < CTX:
        src = g.hT_in if l == 0 else g.HT
        return src.rearrange("(c p) t -> p c t", p=128)[:, :, t0:t0 + n]
    src = g.xT_in if l == 0 else g.XT
    return src.rearrange("(c p) t -> p c t", p=128)[:, :, t0 - CTX:t0 - CTX + n]


def xdst_tile(g, l, t0, n, final):
    if t0 < CTX:
        return g.HT.rearrange("(c p) t -> p c t", p=128)[:, :, t0:t0 + n]
    dst = g.outT if final else g.XT
    return dst.rearrange("(c p) t -> p c t", p=128)[:, :, t0 - CTX:t0 - CTX + n]


def stage_k1(kb, g, l):
    P = kb.P
    main = kb.st
    kb.st = contextlib.ExitStack()
    NJ = N_IN // 128
    wbf, _ = kb.sb([128, 8, N_IN], BF16)
    bwk = [Buf() for _ in range(8)]
    wst = kb.rot_sb(2, [128, N_IN], F32)
    wv = g.w_in[l].rearrange("(c p) n -> p c n", p=128)
    load_w_bf16(kb, wbf, bwk, lambda kc: wv[:, kc, :], 8, N_IN, wst)
    gs, bgs = make_gs(kb, g, l, "g_mix_pre", 1)
    rot = {"sq": kb.rot_sb(1, [128, 8, 512], BF16), "pss": kb.rot_ps(1), "rs": kb.rot_sb(2, [128, 512]), "tmp": kb.rot_sb(2, [128, 512])}
    xr = kb.rot_sb(2, [128, 8, 512])
    xmr = kb.rot_sb(2, [128, 8, 512], BF16)
    pmm = kb.rot_ps(4)
    otr = kb.rot_sb(4, [128, 512])
    ev = 0
    for (t0, n, slo, shi) in SEGS:
        seg = 1 if t0 < CTX else 0
        xt, bx = xr.next()
        DMA_(P, "sp", xt[:, :, 0:n], xsrc_tile(g, l, t0, n), [g.bX], [bx])
        xm, bxm = xmr.next()
        norm_mod_tile(kb, g, l, xt, bx, n, seg, gs, bgs, 0, xm, bxm, rot)
        for j in range(NJ):
            pt, bp = pmm.next()
            for c in range(8):
                MM_(P, pt[:, 0:n], wbf[:, c, j * 128:(j + 1) * 128], xm[:, c, 0:n], c == 0, c == 7, [bwk[c], bxm], [bp])
            ot, bo = otr.next()
            CP_(P, "dve" if ev % 2 == 0 else "act", ot[:, 0:n], pt[:, 0:n], [], [bo, bp])
            ev += 1
            DMA_(P, "sp", g.UT[j * 128:(j + 1) * 128, t0:t0 + n], ot[:, 0:n], [bo], [g.bUT])
    P.barrier()
    kb.st.close()
    kb.st = main


def stage_k2(kb, g, l):
    P = kb.P
    main = kb.st
    kb.st = contextlib.ExitStack()

    def C(name, n, rows=slice(0, 128)):
        return g.cst[rows, CST[name]:CST[name] + n]

    QR, bQR = kb.sb([128, 2, TT], BF16)
    KR, bKR = kb.sb([128, TT], BF16)
    vaug, bva = kb.sb([128, 34, 128], BF16)
    MS_(P, "pool", vaug[:], 1.0, [bva])
    xin = kb.rot_sb(4, [128, 512])
    csr = kb.rot_sb(3, [128, 2, 512])
    t5 = kb.rot_sb(16, [128, 512])
    pq0 = kb.rot_ps(1)
    vtr = kb.rot_sb(4, [64, 128])
    psT = kb.rot_ps(5)
    pq = Rot(pq0.items + psT.items)
    poT = kb.rot_ps(2)
    ptr = kb.rot_sb(5, [128, 512], BF16)
    rdr = kb.rot_sb(2, [64, 512])
    oor = kb.rot_sb(2, [64, 512])
    gne, bgne = kb.sb([128, 1])
    MS_(P, "pool", gne[:], NORM_EPS, [bgne])

    def normrope(src_rows, gain_name, dst_fn, bdst):
        for (t0, n, slo, shi) in SEGS:
            x, bx = xin.next()
            for i, (ps_, r0) in enumerate(src_rows):
                DMA_(P, "sp" if i == 0 else "act", x[ps_, 0:n], g.UT[r0:r0 + (ps_.stop - ps_.start), t0:t0 + n], [g.bUT], [bx])
            sq, bsq = t5.next()
            TT_(P, "pool", sq[:, 0:n], x[:, 0:n], x[:, 0:n], ALU.mult, [bx], [bsq])
            pa, bpa = pq.next()
            MM_(P, pa[:, 0:n], C("blk1", 128), sq[:, 0:n], True, True, [bsq, g.bcst], [bpa])
            rs, brs = t5.next()
            ACT_(P, rs[:, 0:n], pa[:, 0:n], AF.Sqrt, [bgne], [brs, bpa], bias=gne[:], scale=1.0 / 64)
            P.op("dve", lambda e, rs=rs, n=n: e.reciprocal(out=rs[:, 0:n], in_=rs[:, 0:n]), reads=[brs], writes=[brs])
            xn, bxn = t5.next()
            STT_(P, xn[:, 0:n], x[:, 0:n], g.vec[:, l, VEC[gain_name]:VEC[gain_name] + 1], rs[:, 0:n], ALU.mult, ALU.mult, [bx, brs, g.bvec], [bxn])
            if t0 < CTX:
                CP_(P, "act", dst_fn(t0, n), xn[:, 0:n], [bxn], [bdst])
            else:
                cs_, bcs = csr.next()
                DMA_(P, "sp", cs_[:, 0, 0:n], g.cosT[:, t0 - CTX:t0 - CTX + n], [], [bcs])
                DMA_(P, "act", cs_[:, 1, 0:n], g.sinT[:, t0 - CTX:t0 - CTX + n], [], [bcs])
                pb, bpb = pq.next()
                MM_(P, pb[:, 0:n], C("prot", 128), xn[:, 0:n], True, True, [bxn, g.bcst], [bpb])
                t1, bt1 = t5.next()
                TT_(P, "pool", t1[:, 0:n], xn[:, 0:n], cs_[:, 0, 0:n], ALU.mult, [bxn, bcs], [bt1])
                t2, bt2 = t5.next()
                TT_(P, "dve", t2[:, 0:n], pb[:, 0:n], cs_[:, 1, 0:n], ALU.mult, [bcs], [bt2, bpb])
                TT_(P, "dve", dst_fn(t0, n), t1[:, 0:n], t2[:, 0:n], ALU.add, [bt1, bt2], [bdst])

    for gq in range(2):
        for hp2 in range(2):
            normrope([(slice(0, 128), gq * 256 + hp2 * 128)], "qn", lambda t0, n, hp2=hp2: QR[:, hp2, t0:t0 + n], bQR)
        kr0 = 512 + gq * 64
        normrope([(slice(0, 64), kr0), (slice(64, 128), kr0)], "kn", lambda t0, n: KR[:, t0:t0 + n], bKR)
        vr0 = 640 + gq * 64
        for kc in range(34):
            vt, bvt = vtr.next()
            DMA_(P, "sp" if kc % 2 == 0 else "act", vt[:], g.UT[vr0:vr0 + 64, kc * 128:(kc + 1) * 128], [g.bUT], [bvt])
            pv, bpv = pq.next()
            P.op("pe", lambda e, pv=pv, vt=vt: e.transpose(pv[:, 0:64], vt[:], g.cst[0:64, CST["ident"]:CST["ident"] + 64]),
                 reads=[bvt, g.bcst], writes=[bpv])
            CP_(P, "dve", vaug[:, kc, 0:64], pv[:, 0:64], [], [bva, bpv])
        for hh in range(4):
            hsl = slice((hh % 2) * 64, (hh % 2) * 64 + 64)
            hp2 = hh // 2
            row0 = (gq * 4 + hh) * 64
            for (t0, n, slo, shi) in SEGS:
                nk = 2 if t0 < CTX else 34
                po, bpo = poT.next()
                sts = {}
                LOOK = 4

                def issue_s(kc, n=n, t0=t0):
                    ps_, bps = psT.next()
                    MM_(P, ps_[:, 0:n], KR[hsl, kc * 128:(kc + 1) * 128], QR[hsl, hp2, t0:t0 + n], True, True, [bKR, bQR], [bps])
                    sts[kc] = (ps_, bps)
                for kc in range(min(LOOK, nk)):
                    issue_s(kc)
                for kc in range(nk):
                    ps_, bps = sts.pop(kc)
                    pt_, bpt = ptr.next()
                    ACT_(P, pt_[:, 0:n], ps_[:, 0:n], AF.Exp, [], [bpt, bps], scale=0.125)
                    if kc + LOOK < nk:
                        issue_s(kc + LOOK)
                    MM_(P, po[:, 0:n], vaug[:, kc, :], pt_[:, 0:n], kc == 0, kc == nk - 1, [bva, bpt], [bpo])
                rd, brd = rdr.next()
                P.op("dve", lambda e, rd=rd, po=po, n=n: e.reciprocal(out=rd[:, 0:n], in_=po[64:128, 0:n]), reads=[], writes=[brd, bpo])
                oo, boo = oor.next()
                TT_(P, "dve", oo[:, 0:n], po[0:64, 0:n], rd[:, 0:n], ALU.mult, [brd], [boo, bpo])
                DMA_(P, "sp", g.ATT[row0:row0 + 64, t0:t0 + n], oo[:, 0:n], [boo], [g.bATT])
    P.barrier()
    kb.st.close()
    kb.st = main


def epilogue(kb, g, l, src, bsrc, n, seg, t0, gp, bgp, rot, final):
    P = kb.P
    sq, bsq = rot["sq"].next()
    ps_, bps = rot["pss"].next()
    rs, brs = rot["rs"].next()
    rstd8(kb, g, src[:, :, 0:n], bsrc, n, sq[:, :, 0:n], bsq, ps_[:, 0:n], bps, rs[:, 0:n], brs)
    xt, bx = rot["x"].next()
    DMA_(P, "act", xt[:, :, 0:n], rot["xsrc"](t0, n), [g.bX], [bx])
    for j in range(8):
        STT_(P, src[:, j, 0:n], src[:, j, 0:n], gp[:, seg, j:j + 1], rs[:, 0:n], ALU.mult, ALU.mult, [bsrc, bgp, brs], [bsrc])
    TT_(P, "pool", xt[:, :, 0:n], xt[:, :, 0:n], src[:, :, 0:n], ALU.add, [bx, bsrc], [bx])
    DMA_(P, "sp", xdst_tile(g, l, t0, n, final), xt[:, :, 0:n], [bx], [g.bX2])


def stage_k4(kb, g, l):
    P = kb.P
    main = kb.st
    kb.st = contextlib.ExitStack()
    wbf, _ = kb.sb([128, 8, D], BF16)
    bwk = [Buf() for _ in range(8)]
    wst = kb.rot_sb(2, [128, D], F32)
    wv = g.w_out[l].rearrange("(c p) n -> p c n", p=128)
    load_w_bf16(kb, wbf, bwk, lambda kc: wv[:, kc, :], 8, D, wst)
    gp, bgp = make_gp(kb, g, l, "g_mix_post", 2)
    rot = {"sq": kb.rot_sb(1, [128, 8, 512], BF16), "pss": kb.rot_ps(1), "rs": kb.rot_sb(2, [128, 512]),
           "x": kb.rot_sb(2, [128, 8, 512]), "xsrc": lambda t0, n: xsrc_tile(g, l, t0, n)}
    mixr = kb.rot_sb(1, [128, 6, 512])
    cvr = kb.rot_sb(2, [128, 6, 514])
    pr_ = kb.rot_sb(2, [128, 2, 514])
    cnr = kb.rot_sb(2, [128, 2, 512])
    mbr = kb.rot_sb(2, [128, 8, 512], BF16)
    mor = kb.rot_sb(2, [128, 8, 512])
    pmm = kb.rot_ps(4)
    UTcv = g.UT[1920:2688, :].rearrange("(c p) t -> p c t", p=128)
    ATTv = g.ATT.rearrange("(c p) t -> p c t", p=128)
    RWOv = g.RWO.rearrange("(c p) t -> p c t", p=128)
    ev = 0
    for (t0, n, slo, shi) in SEGS:
        seg = 1 if t0 < CTX else 0
        mx, bmx = mixr.next()
        DMA_(P, "sp", mx[:, 0:4, 0:n], ATTv[:, :, t0:t0 + n], [g.bATT], [bmx])
        DMA_(P, "act", mx[:, 4:6, 0:n], RWOv[:, :, t0:t0 + n], [g.bRWO], [bmx])
        cv, bcv = cvr.next()
        MS_(P, "pool", cv[:, :, 0:1], 0.0, [bcv])
        MS_(P, "pool", cv[:, :, n + 1:n + 2], 0.0, [bcv])
        lo, hi = max(t0 - 1, slo), min(t0 + n + 1, shi)
        DMA_(P, "sp", cv[:, :, lo - (t0 - 1):hi - (t0 - 1)], UTcv[:, :, lo:hi], [g.bUT], [bcv])
        pp, bpp = pr_.next()
        TT_(P, "pool", pp[:, :, 0:n + 2], cv[:, 2:4, 0:n + 2], cv[:, 4:6, 0:n + 2], ALU.mult, [bcv], [bpp])
        cn, bcn = cnr.next()
        mb, bmb = mbr.next()
        for cc in range(2):
            sv = lambda i, cc=cc: g.vec[:, l, VEC["sconv"] + i * 2 + cc:VEC["sconv"] + i * 2 + cc + 1]
            TS_(P, "dve", cn[:, cc, 0:n], pp[:, cc, 1:n + 1], sv(1), None, ALU.mult, None, [bpp, g.bvec], [bcn])
            STT_(P, cn[:, cc, 0:n], pp[:, cc, 0:n], sv(0), cn[:, cc, 0:n], ALU.mult, ALU.add, [bpp, g.bvec, bcn], [bcn])
            STT_(P, cn[:, cc, 0:n], pp[:, cc, 2:n + 2], sv(2), cn[:, cc, 0:n], ALU.mult, ALU.add, [bpp, g.bvec, bcn], [bcn])
            TT_(P, "dve", mb[:, 6 + cc, 0:n], cn[:, cc, 0:n], cv[:, cc, 1:n + 1], ALU.mult, [bcn, bcv], [bmb])
        CP_(P, "act", mb[:, 0:6, 0:n], mx[:, :, 0:n], [bmx], [bmb])
        mo, bmo = mor.next()
        for j in range(8):
            pt, bp = pmm.next()
            for c in range(8):
                MM_(P, pt[:, 0:n], wbf[:, c, j * 128:(j + 1) * 128], mb[:, c, 0:n], c == 0, c == 7, [bwk[c], bmb], [bp])
            CP_(P, "dve" if ev % 2 == 0 else "act", mo[:, j, 0:n], pt[:, 0:n], [], [bmo, bp])
            ev += 1
        epilogue(kb, g, l, mo, bmo, n, seg, t0, gp, bgp, rot, False)
    P.barrier()
    kb.st.close()
    kb.st = main


NPAD = TT + 3


def pad_col(t):
    return t + 1 if t < CTX else t + 2


def stage_k5(kb, g, l, final):
    P = kb.P
    main = kb.st
    kb.st = contextlib.ExitStack()
    gs, bgs = make_gs(kb, g, l, "g_ffn_pre", 4)
    hres, bh = kb.sb([128, 8, TT], BF16)
    XM_src = lambda t0, n: (g.HT if t0 < CTX else g.XT).rearrange("(c p) t -> p c t", p=128)[:, :, (t0 if t0 < CTX else t0 - CTX):(t0 if t0 < CTX else t0 - CTX) + n]
    stk = contextlib.ExitStack()
    outer = kb.st
    kb.st = stk
    rot = {"sq": kb.rot_sb(1, [128, 8, 512], BF16), "pss": kb.rot_ps(1), "rs": kb.rot_sb(2, [128, 512]), "tmp": kb.rot_sb(2, [128, 512])}
    xr = kb.rot_sb(2, [128, 8, 512])
    for (t0, n, slo, shi) in SEGS:
        seg = 1 if t0 < CTX else 0
        xt, bx = xr.next()
        DMA_(P, "sp", xt[:, :, 0:n], XM_src(t0, n), [g.bX2], [bx])

        class _V:
            pass
        hv = hres[:, :, t0:t0 + n]
        norm_mod_tile(kb, g, l, xt, bx, n, seg, gs, bgs, 3, hv, bh, rot)
    P.barrier()
    stk.close()
    kb.st = outer
    gar = kb.rot_sb(2, [128, NPAD], BF16)
    upr = kb.rot_sb(2, [128, NPAD], BF16)
    tgr = kb.rot_sb(2, [128, NPAD])
    sgr = kb.rot_sb(1, [128, NPAD], BF16)
    for tl, bt in gar.items + upr.items:
        MS_(P, "pool", tl[:, 0:1], 0.0, [bt])
        MS_(P, "pool", tl[:, CTX + 1:CTX + 2], 0.0, [bt])
        MS_(P, "pool", tl[:, NPAD - 1:NPAD], 0.0, [bt])
    actr = kb.rot_sb(1, [128, NPAD], BF16)
    wst = kb.rot_sb(2, [128, 8, 256], F32)
    wbr = kb.rot_sb(2, [128, 8, 256], BF16)
    pmm = kb.rot_ps(4)
    upv = g.ffn_up[l].rearrange("(c p) n -> p c n", p=128)
    N1 = NPAD - 2
    ev = 0
    for j in range(22):
        ws, bws = wst.next()
        DMA_(P, "sp", ws[:, :, 0:128], upv[:, :, j * 128:(j + 1) * 128], [], [bws])
        DMA_(P, "pool", ws[:, :, 128:256], upv[:, :, D_FF + j * 128:D_FF + (j + 1) * 128], [], [bws])
        wb, bwb = wbr.next()
        CP_(P, "pool", wb[:], ws[:], [bws], [bwb])
        gaT, bga = gar.next()
        upT, bup = upr.next()
        tg, btg = tgr.next()
        sg_, bsg_ = sgr.next()
        for (t0, n, slo, shi) in SEGS:
            pc = pad_col(t0)
            for which, (dst, bdst) in enumerate(((gaT, bga), (upT, bup))):
                pt, bp = pmm.next()
                for c in range(8):
                    MM_(P, pt[:, 0:n], wb[:, c, which * 128:(which + 1) * 128], hres[:, c, t0:t0 + n], c == 0, c == 7, [bwb, bh], [bp])
                CP_(P, "act", dst[:, pc:pc + n], pt[:, 0:n], [], [bdst, bp])
                ev += 1
        def conv3(src, bsrc, which, j=j):
            fv = lambda i: g.vec[:, l, VEC["fconv"] + i * 44 + which * 22 + j:VEC["fconv"] + i * 44 + which * 22 + j + 1]
            ACT_(P, tg[:, 0:N1], src[:, 1:N1 + 1], AF.Identity, [bsrc, g.bvec], [btg], scale=fv(1))
            STT_(P, tg[:, 0:N1], src[:, 0:N1], fv(0), tg[:, 0:N1], ALU.mult, ALU.add, [bsrc, g.bvec, btg], [btg])
            STT_(P, tg[:, 0:N1], src[:, 2:N1 + 2], fv(2), tg[:, 0:N1], ALU.mult, ALU.add, [bsrc, g.bvec, btg], [btg])
        conv3(gaT, bga, 0)
        ACT_(P, sg_[:, 0:N1], tg[:, 0:N1], AF.Silu, [btg], [bsg_])
        conv3(upT, bup, 1)
        ac, bac = actr.next()
        TT_(P, "dve", ac[:, 0:N1], sg_[:, 0:N1], tg[:, 0:N1], ALU.mult, [bsg_, btg], [bac])
        DMA_(P, "sp", g.ACTS[j * 128:(j + 1) * 128, 0:CTX], ac[:, 0:CTX], [bac], [g.bACTS])
        DMA_(P, "act", g.ACTS[j * 128:(j + 1) * 128, CTX:TT], ac[:, CTX + 1:CTX + 1 + SEQ], [bac], [g.bACTS])
    P.barrier()
    kb.st.close()
    kb.st = contextlib.ExitStack()
    wd, _ = kb.sb([128, 22, D], BF16)
    bwd = [Buf() for _ in range(22)]
    wst2 = kb.rot_sb(2, [128, D], F32)
    dv = g.ffn_down[l].rearrange("(j p) n -> p j n", p=128)
    load_w_bf16(kb, wd, bwd, lambda kc: dv[:, kc, :], 22, D, wst2)
    gp, bgp = make_gp(kb, g, l, "g_ffn_post", 5)
    rot = {"sq": kb.rot_sb(1, [128, 8, 512], BF16), "pss": kb.rot_ps(1), "rs": kb.rot_sb(2, [128, 512]),
           "x": kb.rot_sb(2, [128, 8, 512]), "xsrc": XM_src}
    atr = kb.rot_sb(2, [128, 22, 512], BF16)
    fr = kb.rot_sb(2, [128, 8, 512])
    pmm = kb.rot_ps(4)
    AV = g.ACTS.rearrange("(j p) t -> p j t", p=128)
    ev = 0
    g.bX, g.bX2 = g.bX2, g.bX
    for (t0, n, slo, shi) in SEGS:
        seg = 1 if t0 < CTX else 0
        at, bat = atr.next()
        DMA_(P, "sp", at[:, :, 0:n], AV[:, :, t0:t0 + n], [g.bACTS], [bat])
        f, bf = fr.next()
        for nn in range(8):
            pt, bp = pmm.next()
            for j in range(22):
                MM_(P, pt[:, 0:n], wd[:, j, nn * 128:(nn + 1) * 128], at[:, j, 0:n], j == 0, j == 21, [bwd[j], bat], [bp])
            CP_(P, "dve" if ev % 2 == 0 else "act", f[:, nn, 0:n], pt[:, 0:n], [], [bf, bp])
            ev += 1
        epilogue(kb, g, l, f, bf, n, seg, t0, gp, bgp, rot, final)
    g.bX, g.bX2 = g.bX2, g.bX
    P.barrier()
    kb.st.close()
    kb.st = main


def build_mega(nl=DEPTH, dbg=()):
    kb = KB()
    P = kb.P
    nc = kb.nc
    g = G()
    g.nl = nl
    g.scr = {}
    g.xT_in = kb.din("xT", [D, SEQ])
    g.hT_in = kb.din("hT", [D, CTX])
    g.cT = kb.din("cT", [128, 16])
    cst_d = kb.din("cst", [128, NCST])
    vec_d = kb.din("vec", [128, DEPTH * NV])
    g.cosT = kb.din("cosT", [128, SEQ])
    g.sinT = kb.din("sinT", [128, SEQ])
    g.ada_w = kb.din("ada_w", [DEPTH, D, 6 * D])
    g.w_in = kb.din("w_in", [DEPTH, D, N_IN])
    g.w_out = kb.din("w_out", [DEPTH, D, D])
    g.ffn_up = kb.din("ffn_up", [DEPTH, D, 2 * D_FF])
    g.ffn_down = kb.din("ffn_down", [DEPTH, D_FF, D])
    g.w_up = kb.din("w_up", [DEPTH, 128, 256])
    g.a_up = kb.din("a_up", [DEPTH, 128, 256])
    g.g_up = kb.din("g_up", [DEPTH, 128, 256])
    g.outT = kb.dout("outT", [D, SEQ])

    def scr(name, shape, dt=F32):
        return nc.dram_tensor(name, list(shape), dt, kind="Internal").ap()

    g.UT, g.bUT = scr("UT", [N_IN, TT]), Buf()
    g.ATT, g.bATT = scr("ATT", [512, TT]), Buf()
    g.RWO, g.bRWO = scr("RWO", [256, TT]), Buf()
    g.XT = scr("XT", [D, SEQ])
    g.HT = scr("HT", [D, CTX])
    g.bX, g.bX2 = Buf(), Buf()
    g.ACTS, g.bACTS = scr("ACTS", [D_FF, TT], BF16), Buf()
    g.cst, g.bcst = kb.sb([128, NCST])
    g.vec, g.bvec = kb.sb([128, DEPTH, NV])
    g.mt, g.bmt = kb.sb([128, DEPTH, 48, 2])
    g.ones_bf, g.b_ones = kb.sb([128, 128], BF16)
    g.eps, g.b_eps = kb.sb([128, 1])
    DMA_(P, "sp", g.cst[:], cst_d[:, :], [], [g.bcst])
    DMA_(P, "sp", g.vec[:].rearrange("p l v -> p (l v)"), vec_d[:, :], [], [g.bvec])
    MS_(P, "pool", g.ones_bf[:], 1.0, [g.b_ones])
    MS_(P, "pool", g.eps[:], NORM_EPS, [g.b_eps])
    stage_ada(kb, g)
    for l in range(nl):
        final = (l == nl - 1)
        stage_k1(kb, g, l)
        stage_k2(kb, g, l)
        stage_k3(kb, g, l)
        stage_k4(kb, g, l)
        stage_k5(kb, g, l, final)
    for name in dbg:
        src = getattr(g, name)
        o = kb.dout("o_" + name, list(src.shape), src.dtype)
        kb.outs.append(DMA_(P, "sp", o, src, [], []))
    for k_, v_ in g.bX.w.items():
        kb.outs.append((k_, v_))
    for k_, v_ in g.bX2.w.items():
        kb.outs.append((k_, v_))
    return kb.finish()


def make_vec_np(I):
    v = np.zeros((128, DEPTH, NV), np.float32)
    for l in range(DEPTH):
        def put(name, arr):
            v[:, l, VEC[name]:VEC[name] + arr.shape[1]] = arr
        mu = I["rwkv_mu"][l]
        put("mu", np.stack([fm(mu[0], 9), fm(mu[1], 9)], 2).reshape(128, 18))
        put("w0", np.stack([fm(I["rwkv_w0"][l][d], 2) for d in range(2)], 1).reshape(128, 4))
        put("a0", np.stack([fm(I["rwkv_a0"][l][d], 2) for d in range(2)], 1).reshape(128, 4))
        put("k_k", fm(I["rwkv_k_k"][l], 2))
        put("k_a", fm(I["rwkv_k_a"][l], 2))
        put("r_k", fm(I["rwkv_r_k"][l].reshape(256), 2))
        put("lnx_w", fm(I["rwkv_lnx_w"][l], 2))
        put("lnx_b", fm(I["rwkv_lnx_b"][l], 2))
        put("g_mix_pre", fm(I["norm_mix_pre"][l], 8))
        put("g_mix_post", fm(I["norm_mix_post"][l], 8))
        put("g_ffn_pre", fm(I["norm_ffn_pre"][l], 8))
        put("g_ffn_post", fm(I["norm_ffn_post"][l], 8))
        put("qn", np.tile(I["q_norm"][l], 2)[:, None])
        put("kn", np.tile(I["k_norm"][l], 2)[:, None])
        put("sconv", np.stack([fm(I["sconv_w"][l][i], 2) for i in range(3)], 1).reshape(128, 6))
        put("fconv", np.stack([fm(I["ffn_conv"][l][i], 44) for i in range(3)], 1).reshape(128, 132))
        put("ada_b", fm(I["ada_b"][l], 48))
    return v.reshape(128, DEPTH * NV)


def rope_tables():
    t = np.arange(SEQ)
    row = (t // 64).astype(np.float32)
    col = (t % 64).astype(np.float32)
    inv = (np.float32(10000.0) ** (-np.arange(16, dtype=np.float32) / np.float32(16))).astype(np.float32)
    ang = np.concatenate([row[:, None] * inv, col[:, None] * inv], -1).astype(np.float32)
    cos = np.cos(ang).astype(np.float32).T
    sin = np.sin(ang).astype(np.float32).T
    cosf = np.concatenate([cos, cos, cos, cos], 0)
    sinf = np.concatenate([sin, sin, sin, sin], 0)
    return np.ascontiguousarray(cosf), np.ascontiguousarray(sinf)


def make_maps(I):
    vec = make_vec_np(I)
    cst = make_cst_np()
    cosf, sinf = rope_tables()
    shared = {"cst": cst, "vec": vec, "cosT": cosf, "sinT": sinf,
              "ada_w": np.asarray(I["ada_w"], np.float32), "w_in": np.asarray(I["w_in"], np.float32),
              "w_out": np.asarray(I["w_out"], np.float32), "ffn_up": np.asarray(I["ffn_up"], np.float32),
              "ffn_down": np.asarray(I["ffn_down"], np.float32),
              "w_up": np.ascontiguousarray(np.asarray(I["rwkv_w_up"], np.float32).reshape(DEPTH, 128, 256)),
              "a_up": np.ascontiguousarray(np.asarray(I["rwkv_a_up"], np.float32).reshape(DEPTH, 128, 256)),
              "g_up": np.asarray(I["rwkv_g_up"], np.float32)}
    maps = []
    for b in range(BATCH):
        c2 = np.stack([I["c"][b], I["c_ctx"]], 1).astype(np.float32)
        cT = np.ascontiguousarray(c2.reshape(8, 128, 2).transpose(1, 0, 2)).reshape(128, 16)
        m = dict(shared)
        m["xT"] = np.ascontiguousarray(np.asarray(I["x"][b], np.float32).T)
        m["hT"] = np.ascontiguousarray(np.asarray(I["ctx"][b], np.float32).T)
        m["cT"] = cT
        maps.append(m)
    return maps


def kernel(**inputs):
    I = {k: np.asarray(v) for k, v in inputs.items()}
    nc = cached("mega", build_mega)
    maps = make_maps(I)
    res = run_bass_kernel_spmd(nc, maps, core_ids=list(range(BATCH))).results
    out = np.stack([np.ascontiguousarray(res[b]["outT"].T) for b in range(BATCH)], 0)
    return out.astype(np.float32)
```

```python
import contextlib
import numpy as np
import concourse.bass as bass
import concourse.mybir as mybir
from concourse.bass_utils import run_bass_kernel_spmd

F32 = mybir.dt.float32
BF16 = mybir.dt.bfloat16
AF = mybir.ActivationFunctionType
ALU = mybir.AluOpType
AX = mybir.AxisListType

ENGS = ("pe", "dve", "act", "pool", "sp")
NDMASEM = 12

D = 1024
BATCH = 4
SEQ = 4096
DEPTH = 4
CTX = 256
TT = CTX + SEQ
N_IN = 2688
D_FF = 2816
NORM_EPS = 1e-6
GN_EPS = 64e-5
DECAY_SCALE = 0.606531


class Buf:
    __slots__ = ("name", "w", "r")

    def __init__(self, name=""):
        self.name = name
        self.w = {}
        self.r = {}


class _Op:
    __slots__ = ("fn", "waits", "signal", "dma")

    def __init__(self, fn, waits, dma=None):
        self.fn = fn
        self.waits = waits
        self.signal = False
        self.dma = dma


class Prog:
    def __init__(self, nc):
        self.nc = nc
        self.ops = {e: [] for e in ENGS}
        self.snaps = {e: [] for e in ENGS}
        self.seen = {e: {} for e in ENGS}
        self.dma_n = {}
        self.dma_rr = {e: 0 for e in ENGS}

    def _need(self, eng, deps, raw_tokens):
        waits = {}
        for tok in deps:
            if tok is None:
                continue
            k, i = tok
            if k == eng and (tok not in raw_tokens or eng == "pe"):
                continue
            if self.seen[eng].get(k, -1) >= i:
                continue
            if waits.get(k, -1) < i:
                waits[k] = i
        for k, i in waits.items():
            self.seen[eng][k] = max(self.seen[eng].get(k, -1), i)
            if k in ENGS:
                self.ops[k][i].signal = True
                for kk, ii in self.snaps[k][i].items():
                    if self.seen[eng].get(kk, -1) < ii:
                        self.seen[eng][kk] = ii
        return list(waits.items())

    @staticmethod
    def _deps(reads, writes):
        deps = set()
        raw = set()
        for b in reads:
            for t in b.w.items():
                deps.add(t)
                raw.add(t)
        for b in writes:
            deps.update(b.w.items())
            deps.update(b.r.items())
        return deps, raw

    @staticmethod
    def _mark(tok, reads, writes):
        k, i = tok
        for b in reads:
            if b.r.get(k, -1) < i:
                b.r[k] = i
        for b in writes:
            if b.w.get(k, -1) < i:
                b.w[k] = i

    def op(self, eng, fn, reads=(), writes=()):
        deps, raw = self._deps(reads, writes)
        waits = self._need(eng, deps, raw)
        idx = len(self.ops[eng])
        self.ops[eng].append(_Op(fn, waits))
        self.snaps[eng].append(dict(self.seen[eng]))
        tok = (eng, idx)
        self._mark(tok, reads, writes)
        return tok

    def dma(self, q, fn, reads=(), writes=()):
        j = self.dma_rr[q] % NDMASEM
        self.dma_rr[q] += 1
        key = ("dma", q, j)
        n = self.dma_n.get(key, 0)
        deps, raw = self._deps(reads, writes)
        if n > 0:
            deps.add((key, 16 * n))
            raw.add((key, 16 * n))
        waits = self._need(q, deps, raw)
        self.ops[q].append(_Op(fn, waits, dma=key))
        self.snaps[q].append(dict(self.seen[q]))
        self.dma_n[key] = n + 1
        tok = (key, 16 * (n + 1))
        self._mark(tok, reads, writes)
        return tok

    def wait_all(self, eng, toks):
        waits = self._need(eng, set(toks), set(toks))
        self.ops[eng].append(_Op(None, waits))
        self.snaps[eng].append(dict(self.seen[eng]))

    def barrier(self):
        toks = []
        for e in ENGS:
            for i in range(len(self.ops[e]) - 1, -1, -1):
                o = self.ops[e][i]
                if o.fn is not None and o.dma is None:
                    toks.append((e, i))
                    break
        for key, n in self.dma_n.items():
            toks.append((key, 16 * n))
        for e in ENGS:
            self.wait_all(e, toks)

    def emit(self):
        nc = self.nc
        val = {}
        for e in ENGS:
            c = 0
            v = []
            for o in self.ops[e]:
                if o.signal and o.dma is None and o.fn is not None:
                    c += 1
                v.append(c)
            val[e] = v
        with contextlib.ExitStack() as st:
            sems = {}
            for e in ENGS:
                sems[e] = st.enter_context(nc.semaphore("s_" + e))
            for key in self.dma_n:
                sems[key] = st.enter_context(nc.semaphore("d_%s_%d" % (key[1], key[2])))
            block = st.enter_context(nc.Block())
            handles = {"pe": block.tensor, "dve": block.vector, "act": block.scalar,
                       "pool": block.gpsimd, "sp": block.sync}

            def mk(e):
                def body(eng):
                    for o in self.ops[e]:
                        for k, ii in o.waits:
                            eng.wait_ge(sems[k], val[k][ii] if k in ENGS else ii)
                        if o.fn is None:
                            continue
                        inst = o.fn(eng)
                        if o.dma is not None:
                            inst.then_inc(sems[o.dma], 16)
                        elif o.signal:
                            inst.then_inc(sems[e], 1)
                return body

            for e in ENGS:
                if self.ops[e]:
                    handles[e](mk(e))
        return nc


class Rot:
    def __init__(self, items):
        self.items = items
        self.i = 0

    def next(self):
        it = self.items[self.i % len(self.items)]
        self.i += 1
        return it


class KB:
    def __init__(self):
        self.nc = bass.Bass("TRN2", target_bir_lowering=False)
        self.st = contextlib.ExitStack()
        self.P = Prog(self.nc)
        self.outs = []
        self._n = 0

    def din(self, name, shape, dt=F32):
        return self.nc.dram_tensor(name, list(shape), dt, kind="ExternalInput").ap()

    def dout(self, name, shape, dt=F32):
        return self.nc.dram_tensor(name, list(shape), dt, kind="ExternalOutput").ap()

    def sb(self, shape, dt=F32, name=None):
        self._n += 1
        t = self.st.enter_context(self.nc.sbuf_tensor(name or "t%d" % self._n, list(shape), dt))
        return t, Buf(name or "t%d" % self._n)

    def ps(self, shape=(128, 512), dt=F32, name=None):
        self._n += 1
        t = self.st.enter_context(self.nc.psum_tensor(name or "p%d" % self._n, list(shape), dt))
        return t, Buf(name or "p%d" % self._n)

    def rot_sb(self, n, shape, dt=F32):
        return Rot([self.sb(shape, dt) for _ in range(n)])

    def rot_ps(self, n, shape=(128, 512), dt=F32):
        return Rot([self.ps(shape, dt) for _ in range(n)])

    def load(self, dst, src, b, q="sp"):
        return self.P.dma(q, lambda e: e.dma_start(out=dst, in_=src), writes=[b])

    def store(self, dst, src, b, q="sp"):
        tok = self.P.dma(q, lambda e: e.dma_start(out=dst, in_=src), reads=[b])
        self.outs.append(tok)
        return tok

    def finish(self):
        self.P.wait_all("sp", self.outs)
        self.P.emit()
        self.st.close()
        return self.nc


def run(nc, in_maps):
    res = run_bass_kernel_spmd(nc, in_maps, core_ids=list(range(8)))
    return res.results


_CACHE = {}


def cached(name, fn):
    if name not in _CACHE:
        _CACHE[name] = fn()
    return _CACHE[name]


def fm(v, nchunk):
    return np.ascontiguousarray(np.asarray(v, np.float32).reshape(nchunk, 128).T)


VEC = {}
_o = 0
for _name, _n in [("mu", 18), ("w0", 4), ("a0", 4), ("k_k", 2), ("k_a", 2), ("r_k", 2), ("lnx_w", 2), ("lnx_b", 2),
                  ("g_mix_pre", 8), ("g_mix_post", 8), ("g_ffn_pre", 8), ("g_ffn_post", 8), ("qn", 1), ("kn", 1),
                  ("sconv", 6), ("fconv", 132), ("ada_b", 48)]:
    VEC[_name] = _o
    _o += _n
NV = _o

CST = {}
_o = 0
for _name, _n in [("ident", 128), ("blk1", 128), ("mask4", 512), ("mL", 128), ("mask01", 512), ("ident2", 64),
                  ("prot", 128)]:
    CST[_name] = _o
    _o += _n
NCST = _o


def make_cst_np():
    c = np.zeros((128, NCST), np.float32)
    I = np.eye(128, dtype=np.float32)
    c[:, CST["ident"]:CST["ident"] + 128] = I
    blk = np.zeros((128, 128), np.float32)
    blk[0:64, 0:64] = 1
    blk[64:, 64:] = 1
    c[:, CST["blk1"]:CST["blk1"] + 128] = blk
    s = np.arange(128)[:, None]
    t = np.arange(128)[None, :]
    mUs = blk * (s < t)
    mUi = blk * (s <= t)
    c[:, CST["mask4"]:CST["mask4"] + 512] = np.concatenate([mUs, mUi, mUs, mUi], 1)
    c[:, CST["mL"]:CST["mL"] + 128] = blk * (s > t)
    m01 = np.ones((128, 512), np.float32)
    m01[:, ::64] = 0
    c[:, CST["mask01"]:CST["mask01"] + 512] = m01
    c[:, CST["ident2"]:CST["ident2"] + 64] = np.concatenate([np.eye(64), np.eye(64)], 0)
    pr = np.zeros((128, 128), np.float32)
    for h in range(2):
        for i in range(64):
            if i < 32:
                pr[h * 64 + i + 32, h * 64 + i] = -1.0
            else:
                pr[h * 64 + i - 32, h * 64 + i] = 1.0
    c[:, CST["prot"]:CST["prot"] + 128] = pr
    return c


SEGS = [(0, 256, 0, 256)] + [(256 + 512 * i, 512, 256, TT) for i in range(8)]


class G:
    pass


def stage_k3(kb, g, l):
    P = kb.P
    nc = kb.nc
    cst, bcst = g.cst, g.bcst
    vec, bvec = g.vec, g.bvec

    def C(name, n, rows=slice(0, 128)):
        return cst[rows, CST[name]:CST[name] + n]

    def Vv(name, i=0, n=1):
        o = VEC[name] + i
        return vec[:, l, o:o + n]

    main_st = kb.st
    dscr = {}

    def scratch(name, shape):
        if name not in g.scr:
            g.scr[name] = (nc.dram_tensor(name, list(shape), F32, kind="Internal").ap(), Buf(name))
        return g.scr[name]

    R_, bR = scratch("R", [256, TT])
    V_, bV = scratch("V", [256, TT])
    KK_, bKK = scratch("KK", [256, TT])
    G_, bG = scratch("G", [256, TT])
    BON_, bBON = scratch("BON", [256, TT])
    KE_ = [scratch("KE%d" % d, [256, TT]) for d in range(2)]
    KKA_ = [scratch("KKA%d" % d, [256, TT]) for d in range(2)]
    LW_ = [scratch("LW%d" % d, [256, TT]) for d in range(2)]
    YD_ = [scratch("YD%d" % d, [256, TT]) for d in range(2)]

    kb.st = contextlib.ExitStack()
    wup, bwup = kb.sb([128, 256])
    aup, baup = kb.sb([128, 256])
    gup, bgup = kb.sb([128, 256])
    kb.load(wup[:], g.w_up[l], bwup)
    kb.load(aup[:], g.a_up[l], baup)
    kb.load(gup[:], g.g_up[l], bgup)
    c0, bc0 = kb.sb([128, 9])
    muv = Vv("mu", 0, 18).rearrange("p (c i) -> p c i", i=2)
    P.op("dve", lambda e: e.tensor_tensor(out=c0[:], in0=muv[:, :, 0], in1=muv[:, :, 1], op=ALU.add), reads=[bvec], writes=[bc0])
    P.op("dve", lambda e: e.tensor_scalar(out=c0[:], in0=c0[:], scalar1=-1.0, scalar2=1.0, op0=ALU.mult, op1=ALU.add),
         reads=[bc0], writes=[bc0])
    tiny, btiny = kb.sb([128, 1])
    P.op("pool", lambda e: e.memset(tiny[:], 0.0), writes=[btiny])
    zinr = kb.rot_sb(2, [128, 9, 514])
    zsr = Rot([(kb.sb([128, 9, 512])[0], [Buf() for _ in range(9)]) for _ in range(2)])
    t512 = kb.rot_sb(24, [128, 512])
    atr = kb.rot_sb(2, [128, 2, 2, 512])
    kkr_ = kb.rot_sb(2, [128, 2, 512])
    pmm = kb.rot_ps(4)
    pbr = kb.rot_ps(2)
    UTrw = g.UT[768:1920, :].rearrange("(c p) t -> p c t", p=128)
    for (t0, n, slo, shi) in SEGS:
        zin, bzin = zinr.next()
        P.op("pool", lambda e, zin=zin: e.memset(zin[:, :, 0:1], 0.0), writes=[bzin])
        P.op("pool", lambda e, zin=zin, n=n: e.memset(zin[:, :, n + 1:n + 2], 0.0), writes=[bzin])
        lo, hi = max(t0 - 1, slo), min(t0 + n + 1, shi)
        kb.P.dma("sp", lambda e, zin=zin, lo=lo, hi=hi, t0=t0: e.dma_start(
            out=zin[:, :, lo - (t0 - 1):hi - (t0 - 1)], in_=UTrw[:, :, lo:hi]), reads=[g.bUT], writes=[bzin])
        zs, bzs = zsr.next()
        for c in range(9):
            P.op("pool", lambda e, c=c, zs=zs, zin=zin, n=n: e.tensor_scalar(
                out=zs[:, c, 0:n], in0=zin[:, c, 1:n + 1], scalar1=c0[:, c:c + 1], scalar2=0.0, op0=ALU.mult, op1=ALU.add),
                reads=[bzin, bc0], writes=[bzs[c]])
        for i, off in ((0, 0), (1, 2)):
            for c in range(9):
                P.op("dve", lambda e, c=c, zs=zs, zin=zin, n=n, i=i, off=off: e.scalar_tensor_tensor(
                    out=zs[:, c, 0:n], in0=zin[:, c, off:off + n], scalar=muv[:, c, i:i + 1], in1=zs[:, c, 0:n],
                    op0=ALU.mult, op1=ALU.add), reads=[bzin, bvec, bzs[c]], writes=[bzs[c]])
        P.dma("sp", lambda e, zs=zs, t0=t0, n=n: e.dma_start(
            out=R_[:, t0:t0 + n].rearrange("(c p) t -> p c t", p=128), in_=zs[:, 0:2, 0:n]), reads=[bzs[0], bzs[1]], writes=[bR])
        P.dma("sp", lambda e, zs=zs, t0=t0, n=n: e.dma_start(
            out=V_[:, t0:t0 + n].rearrange("(c p) t -> p c t", p=128), in_=zs[:, 4:6, 0:n]), reads=[bzs[4], bzs[5]], writes=[bV])
        tw, btw = t512.next()
        P.op("act", lambda e, tw=tw, zs=zs, n=n: e.activation(out=tw[:, 0:n], in_=zs[:, 6, 0:n], func=AF.Tanh), reads=[bzs[6]], writes=[btw])
        at, bat = atr.next()
        for d in range(2):
            ds_ = slice(d * 64, d * 64 + 64)
            for cc in range(2):
                cs_ = slice(cc * 128, cc * 128 + 128)
                pw, bpw = pmm.next()
                P.op("pe", lambda e, pw=pw, tw=tw, ds_=ds_, cs_=cs_, n=n: e.matmul(
                    pw[:, 0:n], lhsT=wup[ds_, cs_], rhs=tw[ds_, 0:n], start=True, stop=True), reads=[bwup, btw], writes=[bpw])
                lwt, blwt = t512.next()
                P.op("act", lambda e, pw=pw, lwt=lwt, d=d, cc=cc, n=n: e.activation(
                    out=lwt[:, 0:n], in_=pw[:, 0:n], func=AF.Sigmoid, bias=Vv("w0", d * 2 + cc), scale=1.0),
                    reads=[bpw, bvec], writes=[blwt])
                P.op("dve", lambda e, lwt=lwt, n=n: e.tensor_scalar(
                    out=lwt[:, 0:n], in0=lwt[:, 0:n], scalar1=-DECAY_SCALE, scalar2=None, op0=ALU.mult), reads=[blwt], writes=[blwt])
                P.dma("sp", lambda e, lwt=lwt, d=d, cs_=cs_, t0=t0, n=n: e.dma_start(
                    out=LW_[d][0][cs_, t0:t0 + n], in_=lwt[:, 0:n]), reads=[blwt], writes=[LW_[d][1]])
                pa, bpa = pmm.next()
                P.op("pe", lambda e, pa=pa, zs=zs, ds_=ds_, cs_=cs_, n=n: e.matmul(
                    pa[:, 0:n], lhsT=aup[ds_, cs_], rhs=zs[ds_, 7, 0:n], start=True, stop=True), reads=[baup, bzs[7]], writes=[bpa])
                P.op("act", lambda e, pa=pa, at=at, d=d, cc=cc, n=n: e.activation(
                    out=at[:, d, cc, 0:n], in_=pa[:, 0:n], func=AF.Sigmoid, bias=Vv("a0", d * 2 + cc), scale=1.0),
                    reads=[bpa, bvec], writes=[bat])
        sg, bsg = t512.next()
        P.op("act", lambda e, sg=sg, zs=zs, n=n: e.activation(out=sg[:, 0:n], in_=zs[:, 8, 0:n], func=AF.Sigmoid), reads=[bzs[8]], writes=[bsg])
        for cc in range(2):
            cs_ = slice(cc * 128, cc * 128 + 128)
            pg, bpg = pmm.next()
            P.op("pe", lambda e, pg=pg, sg=sg, cs_=cs_, n=n: e.matmul(pg[:, 0:n], lhsT=gup[:, cs_], rhs=sg[:, 0:n], start=True, stop=True),
                 reads=[bgup, bsg], writes=[bpg])
            gt, bgt = t512.next()
            P.op("act", lambda e, pg=pg, gt=gt, n=n: e.activation(out=gt[:, 0:n], in_=pg[:, 0:n], func=AF.Copy), reads=[bpg], writes=[bgt])
            P.dma("sp", lambda e, gt=gt, cs_=cs_, t0=t0, n=n: e.dma_start(out=G_[cs_, t0:t0 + n], in_=gt[:, 0:n]), reads=[bgt], writes=[bG])
        kkt, bkkt = kkr_.next()
        for cc in range(2):
            kr, bkr = t512.next()
            P.op("dve", lambda e, kr=kr, zs=zs, cc=cc, n=n: e.tensor_scalar(
                out=kr[:, 0:n], in0=zs[:, 2 + cc, 0:n], scalar1=Vv("k_k", cc), scalar2=None, op0=ALU.mult), reads=[bzs[2 + cc], bvec], writes=[bkr])
            sq, bsq = t512.next()
            P.op("pool", lambda e, kr=kr, sq=sq, n=n: e.tensor_tensor(out=sq[:, 0:n], in0=kr[:, 0:n], in1=kr[:, 0:n], op=ALU.mult), reads=[bkr], writes=[bsq])
            pss, bpss = pmm.next()
            P.op("pe", lambda e, pss=pss, sq=sq, n=n: e.matmul(pss[:, 0:n], lhsT=C("blk1", 128), rhs=sq[:, 0:n], start=True, stop=True),
                 reads=[bcst, bsq], writes=[bpss])
            rn, brn = t512.next()
            P.op("dve", lambda e, rn=rn, pss=pss, n=n: e.tensor_scalar(out=rn[:, 0:n], in0=pss[:, 0:n], scalar1=1e-12, scalar2=None, op0=ALU.max),
                 reads=[bpss], writes=[brn])
            P.op("act", lambda e, rn=rn, n=n: e.activation(out=rn[:, 0:n], in_=rn[:, 0:n], func=AF.Sqrt), reads=[brn], writes=[brn])
            P.op("dve", lambda e, rn=rn, n=n: e.reciprocal(out=rn[:, 0:n], in_=rn[:, 0:n]), reads=[brn], writes=[brn])
            P.op("dve", lambda e, rn=rn, kr=kr, kkt=kkt, cc=cc, n=n: e.tensor_tensor(out=kkt[:, cc, 0:n], in0=kr[:, 0:n], in1=rn[:, 0:n], op=ALU.mult),
                 reads=[brn, bkr], writes=[bkkt])
        P.dma("sp", lambda e, kkt=kkt, t0=t0, n=n: e.dma_start(
            out=KK_[:, t0:t0 + n].rearrange("(c p) t -> p c t", p=128), in_=kkt[:, :, 0:n]), reads=[bkkt], writes=[bKK])
        for cc in range(2):
            cs_ = slice(cc * 128, cc * 128 + 128)
            pb, bpb = pbr.next()
            for d in range(2):
                t1, bt1 = t512.next()
                P.op("dve", lambda e, t1=t1, at=at, d=d, cc=cc, n=n: e.tensor_scalar(
                    out=t1[:, 0:n], in0=at[:, d, cc, 0:n], scalar1=-1.0, scalar2=Vv("k_a", cc), op0=ALU.add, op1=ALU.mult),
                    reads=[bat, bvec], writes=[bt1])
                ke, bke = t512.next()
                P.op("dve", lambda e, t1=t1, ke=ke, zs=zs, cc=cc, n=n: e.scalar_tensor_tensor(
                    out=ke[:, 0:n], in0=t1[:, 0:n], scalar=1.0, in1=zs[:, 2 + cc, 0:n], op0=ALU.add, op1=ALU.mult),
                    reads=[bt1, bzs[2 + cc]], writes=[bke])
                P.dma("sp", lambda e, ke=ke, d=d, cs_=cs_, t0=t0, n=n: e.dma_start(out=KE_[d][0][cs_, t0:t0 + n], in_=ke[:, 0:n]),
                      reads=[bke], writes=[KE_[d][1]])
                ka, bka = t512.next()
                P.op("pool", lambda e, ka=ka, kkt=kkt, at=at, d=d, cc=cc, n=n: e.tensor_tensor(
                    out=ka[:, 0:n], in0=kkt[:, cc, 0:n], in1=at[:, d, cc, 0:n], op=ALU.mult), reads=[bkkt, bat], writes=[bka])
                P.dma("sp", lambda e, ka=ka, d=d, cs_=cs_, t0=t0, n=n: e.dma_start(out=KKA_[d][0][cs_, t0:t0 + n], in_=ka[:, 0:n]),
                      reads=[bka], writes=[KKA_[d][1]])
                pr, bpr = t512.next()
                P.op("dve", lambda e, pr=pr, ke=ke, zs=zs, cc=cc, n=n: e.scalar_tensor_tensor(
                    out=pr[:, 0:n], in0=zs[:, cc, 0:n], scalar=Vv("r_k", cc), in1=ke[:, 0:n], op0=ALU.mult, op1=ALU.mult),
                    reads=[bke, bzs[cc], bvec], writes=[bpr])
                P.op("pe", lambda e, pb=pb, pr=pr, d=d, n=n: e.matmul(pb[:, 0:n], lhsT=C("blk1", 128), rhs=pr[:, 0:n], start=(d == 0), stop=(d == 1)),
                     reads=[bcst, bpr], writes=[bpb])
            bo, bbo = t512.next()
            P.op("dve", lambda e, bo=bo, pb=pb, zs=zs, cc=cc, n=n: e.tensor_tensor(out=bo[:, 0:n], in0=pb[:, 0:n], in1=zs[:, 4 + cc, 0:n], op=ALU.mult),
                 reads=[bpb, bzs[4 + cc]], writes=[bbo])
            P.dma("sp", lambda e, bo=bo, cs_=cs_, t0=t0, n=n: e.dma_start(out=BON_[cs_, t0:t0 + n], in_=bo[:, 0:n]), reads=[bbo], writes=[bBON])
    P.barrier()
    kb.st.close()

    kb.st = contextlib.ExitStack()
    srcr = [kb.rot_sb(1, [128, 512]) for _ in range(6)]
    der = kb.rot_sb(8, [128, 512])
    hatr = [kb.rot_sb(1, [128, 512], BF16 if i_ in (2, 3) else F32) for i_ in range(7)]
    hatb = [kb.rot_sb(1, [128, 512], BF16) for _ in range(2)]
    wcr = kb.rot_sb(1, [128, 8])
    ynr = kb.rot_sb(1, [128, 512])
    A4r = kb.rot_sb(4, [128, 512], BF16)
    TOKr = kb.rot_sb(4, [128, 4, 128], BF16)
    sqset = [[[kb.sb([128, 2, 128], BF16) for _ in range(3)] for _ in range(6)] for _ in range(2)]
    MTr = kb.rot_sb(4, [128, 2, 128])
    Gsr = kb.rot_sb(4, [128, 2, 128])
    R2r = kb.rot_sb(4, [128, 128])
    Y0r = kb.rot_sb(4, [128, 128])
    STr = kb.rot_sb(2, [128, 128])
    for (tl, bt) in MTr.items + Gsr.items + STr.items:
        P.op("pool", lambda e, tl=tl: e.memset(tl[:], 0.0), writes=[bt])
    identb, bidentb = kb.sb([128, 128], BF16)
    P.op("pool", lambda e: e.tensor_copy(out=identb[:], in_=cst[:, CST["ident"]:CST["ident"] + 128]), reads=[bcst], writes=[bidentb])
    pool = kb.rot_ps(5)
    pMGs = [kb.ps() for _ in range(2)]
    pSQ = kb.ps()[0]
    bpSQ = Buf()
    pSQslots = Rot([(pSQ[:, i * 128:(i + 1) * 128], bpSQ) for i in range(4)])
    Zrh = [kb.rot_sb(3, [128, 2, 128], BF16) for _ in range(2)]
    m2rh = [kb.rot_sb(3, [128, 2, 64], BF16) for _ in range(4)]
    seti = 0
    ev = [0]

    def evac_copy(out, in_, reads, writes):
        ev[0] += 1
        if ev[0] % 2:
            P.op("act", lambda e: e.activation(out=out, in_=in_, func=AF.Copy), reads=reads, writes=writes)
        else:
            P.op("dve", lambda e: e.tensor_copy(out=out, in_=in_), reads=reads, writes=writes)

    for d in range(2):
        order = SEGS if d == 0 else [SEGS[0]] + SEGS[:0:-1]
        for hp in range(2):
            rows = slice(hp * 128, hp * 128 + 128)
            ST, bST = STr.next()
            P.op("pool", lambda e, ST=ST: e.memset(ST[:], 0.0), writes=[bST])
            stt_ = [(ST, bST)]
            for (t0, n, slo, shi) in order:
                nch = n // 64

                def rv(ap, n=n, d=d):
                    a = ap[:, 0:n]
                    return a[:, ::-1] if d == 1 else a

                srcs = []
                for i, (dr, bdr) in enumerate([(R_, bR), KE_[d], (V_, bV), (KK_, bKK), KKA_[d], LW_[d]]):
                    tl, btl = srcr[i].next()
                    P.dma("sp" if i % 2 == 0 else "act", lambda e, tl=tl, dr=dr, rows=rows, t0=t0, n=n: e.dma_start(out=tl[:, 0:n], in_=dr[rows, t0:t0 + n]),
                          reads=[bdr], writes=[btl])
                    srcs.append((tl, btl))
                (r_s, br_s), (ke_s, bke_s), (v_s, bv_s), (kk_s, bkk_s), (kka_s, bkka_s), (lw_s, blw_s) = srcs
                lwS, blwS = der.next()
                P.op("pool", lambda e, lwS=lwS, lw_s=lw_s, rv=rv, n=n: e.tensor_copy(out=lwS[:, 0:n], in_=rv(lw_s)), reads=[blw_s], writes=[blwS])
                L, bL = der.next()
                P.op("dve", lambda e, L=L, lwS=lwS, n=n: e.tensor_tensor_scan(
                    out=L[:, 0:n], data0=C("mask01", n), data1=lwS[:, 0:n], initial=0.0, op0=ALU.mult, op1=ALU.add),
                    reads=[blwS, bcst], writes=[bL])
                Lex, bLex = der.next()
                P.op("pool", lambda e, Lex=Lex, L=L, lwS=lwS, n=n: e.tensor_tensor(out=Lex[:, 0:n], in0=L[:, 0:n], in1=lwS[:, 0:n], op=ALU.subtract),
                     reads=[bL, blwS], writes=[bLex])
                Lc, bLc = der.next()
                L3 = L[:, 0:n].rearrange("p (c j) -> p c j", j=64)
                P.op("dve", lambda e, Lc=Lc, L3=L3, n=n, nch=nch: e.tensor_tensor(
                    out=Lc[:, 0:n].rearrange("p (c j) -> p c j", j=64), in0=L3[:, :, 63:64].broadcast_to([128, nch, 64]), in1=L3,
                    op=ALU.subtract), reads=[bL], writes=[bLc])
                eL, beL = der.next()
                eLm, beLm = der.next()
                eLex, beLex = der.next()
                eLc, beLc = der.next()
                WC, bWC = wcr.next()
                P.op("act", lambda e, eL=eL, L=L, n=n: e.activation(out=eL[:, 0:n], in_=L[:, 0:n], func=AF.Exp), reads=[bL], writes=[beL])
                P.op("act", lambda e, eLm=eLm, L=L, n=n: e.activation(out=eLm[:, 0:n], in_=L[:, 0:n], func=AF.Exp, scale=-1.0), reads=[bL], writes=[beLm])
                P.op("act", lambda e, eLex=eLex, Lex=Lex, n=n: e.activation(out=eLex[:, 0:n], in_=Lex[:, 0:n], func=AF.Exp), reads=[bLex], writes=[beLex])
                P.op("act", lambda e, eLc=eLc, Lc=Lc, n=n: e.activation(out=eLc[:, 0:n], in_=Lc[:, 0:n], func=AF.Exp), reads=[bLc], writes=[beLc])
                P.op("act", lambda e, WC=WC, L3=L3, nch=nch: e.activation(out=WC[:, 0:nch], in_=L3[:, :, 63], func=AF.Exp), reads=[bL], writes=[bWC])
                hats = [h_.next() for h_ in hatr]
                (AhT, bAh), (RhT, bRh), (BhT, bBh), (KhT, bKh), (BtT, bBt), (KtT, bKt), (vS, bvS) = hats
                P.op("dve", lambda e, AhT=AhT, kk_s=kk_s, eLex=eLex, rv=rv, n=n: e.scalar_tensor_tensor(
                    out=AhT[:, 0:n], in0=rv(kk_s), scalar=-1.0, in1=eLex[:, 0:n], op0=ALU.mult, op1=ALU.mult), reads=[bkk_s, beLex], writes=[bAh])
                P.op("pool", lambda e, RhT=RhT, r_s=r_s, eL=eL, rv=rv, n=n: e.tensor_tensor(out=RhT[:, 0:n], in0=rv(r_s), in1=eL[:, 0:n], op=ALU.mult),
                     reads=[br_s, beL], writes=[bRh])
                P.op("dve", lambda e, BhT=BhT, kka_s=kka_s, eLm=eLm, rv=rv, n=n: e.tensor_tensor(out=BhT[:, 0:n], in0=rv(kka_s), in1=eLm[:, 0:n], op=ALU.mult),
                     reads=[bkka_s, beLm], writes=[bBh])
                P.op("pool", lambda e, KhT=KhT, ke_s=ke_s, eLm=eLm, rv=rv, n=n: e.tensor_tensor(out=KhT[:, 0:n], in0=rv(ke_s), in1=eLm[:, 0:n], op=ALU.mult),
                     reads=[bke_s, beLm], writes=[bKh])
                P.op("dve", lambda e, BtT=BtT, kka_s=kka_s, eLc=eLc, rv=rv, n=n: e.tensor_tensor(out=BtT[:, 0:n], in0=rv(kka_s), in1=eLc[:, 0:n], op=ALU.mult),
                     reads=[bkka_s, beLc], writes=[bBt])
                P.op("pool", lambda e, KtT=KtT, ke_s=ke_s, eLc=eLc, rv=rv, n=n: e.tensor_tensor(out=KtT[:, 0:n], in0=rv(ke_s), in1=eLc[:, 0:n], op=ALU.mult),
                     reads=[bke_s, beLc], writes=[bKt])
                P.op("pool", lambda e, vS=vS, v_s=v_s, rv=rv, n=n: e.tensor_copy(out=vS[:, 0:n], in_=rv(v_s)), reads=[bv_s], writes=[bvS])
                (AhTb, bAhb), (RhTb, bRhb) = [h_.next() for h_ in hatb]
                CP_(P, "act", AhTb[:, 0:n], AhT[:, 0:n], [bAh], [bAhb])
                CP_(P, "act", RhTb[:, 0:n], RhT[:, 0:n], [bRh], [bRhb])
                Yn, bYn = ynr.next()
                Yv = Yn[:, 0:n][:, ::-1] if d == 1 else Yn[:, 0:n]
                blk3 = C("blk1", 128).rearrange("p (c j) -> p c j", j=64)
                pending = [[]]

                def cp_gen(gi, pMG, bpMG, tr, TOK, bTOK, AhT=AhTb, bAh=bAhb, RhT=RhTb, bRh=bRhb, BhT=BhT, bBh=bBh, KhT=KhT, bKh=bKh):
                    HS = [slice(0, 64), slice(64, 128)]
                    S = sqset[gi]
                    ident2h = C("ident", 128).unsqueeze(1).broadcast_to([128, 2, 128])
                    mL2h = C("mL", 128).unsqueeze(1).broadcast_to([128, 2, 128])
                    A4s = []
                    for h in range(2):
                        hs = HS[h]
                        pa, bpa = pool.next()
                        for qi, (lt, blt, rt, brt) in enumerate([(BhT, bBh, AhT, bAh), (BhT, bBh, RhT, bRh), (KhT, bKh, AhT, bAh), (KhT, bKh, RhT, bRh)]):
                            MM_(P, pa[:, qi * 128:(qi + 1) * 128], lt[hs, tr], rt[hs, tr], True, True, [blt, brt], [bpa])
                        A4, bA4 = A4r.next()
                        TT_(P, "dve", A4[:], pa[:], C("mask4", 512), ALU.mult, [bcst], [bA4, bpa])
                        A4s.append((A4, bA4))
                    (N0, bN0), (NT0, bNT0), (Ip0, bIp0) = S[0]
                    pxs = [pool.next(), pool.next()]
                    for h in range(2):
                        px, bpx = pxs[h]
                        MM_(P, px[:, 0:128], AhT[HS[h], tr], BhT[HS[h], tr], True, True, [bAh, bBh], [bpx])
                    for h in range(2):
                        px, bpx = pxs[h]
                        TT_(P, "dve", N0[:, h, :], px[:, 0:128], C("mL", 128), ALU.mult, [bcst], [bN0, bpx])
                    yield
                    pk, bpk = pxs[0]
                    for h in range(2):
                        MM_(P, pk[:, 128 + h * 64:192 + h * 64], A4s[h][0][:, 256:384], TOK[:, 3, HS[h]], True, True, [A4s[h][1], bTOK], [bpk])
                    Z, bZ = Zrh[gi].next()
                    CP_(P, "pool", Z[:, :, 0:64], TOK[:, 0, :].rearrange("p (h k) -> p h k", k=64), [bTOK], [bZ])
                    CP_(P, "dve", Z[:, :, 64:128], pk[:, 128:256].rearrange("p (h k) -> p h k", k=64), [], [bZ, bpk])
                    yield
                    for i in range(6):
                        (Ni, bNi), (NTi, bNTi), (Ipi, bIpi) = S[i]

                        def NT_ap(h, i=i, NTi=NTi):
                            return A4s[h][0][:, 0:128] if i == 0 else NTi[:, h, :]

                        def NT_b(h, i=i, bNTi=bNTi):
                            return A4s[h][1] if i == 0 else bNTi
                        L1, bL1 = pool.next()
                        for h in range(2):
                            MM_(P, L1[:, h * 128:(h + 1) * 128], NT_ap(h), Z[:, h, :], True, False, [NT_b(h), bZ], [bL1])
                            MM_(P, L1[:, h * 128:(h + 1) * 128], identb[:], Z[:, h, :], False, True, [bidentb, bZ], [bL1])
                        if i < 5:
                            (Nn, bNn), (NTn, bNTn), (Ipn, bIpn) = S[i + 1]
                            for h in range(2):
                                MM_(P, L1[:, 256 + h * 128:256 + (h + 1) * 128], Ni[:, h, :], NT_ap(h), True, True, [bNi, NT_b(h)], [bL1])
                        if i < 4:
                            L2, bL2 = pool.next()
                            for h in range(2):
                                MM_(P, L2[:, h * 128:(h + 1) * 128], NT_ap(h), Ni[:, h, :], True, True, [bNi, NT_b(h)], [bL2])
                        Z2, bZ2 = Zrh[gi].next()
                        evac_copy(Z2[:].rearrange("p h k -> p (h k)"), L1[:, 0:256], [], [bZ2, bL1])
                        Z, bZ = Z2, bZ2
                        if i < 5:
                            evac_copy(NTn[:].rearrange("p h k -> p (h k)"), L1[:, 256:512], [], [bNTn, bL1])
                        if i < 4:
                            CP_(P, "act", Nn[:].rearrange("p h k -> p (h k)"), L2[:, 0:256], [], [bNn, bL2])
                        yield
                    m2 = []
                    for h in range(2):
                        hs = HS[h]
                        B2, bB2 = m2rh[gi * 2 + h].next()
                        V2, bV2 = m2rh[gi * 2 + h].next()
                        Q2, bQ2 = m2rh[gi * 2 + h].next()
                        TT_(P, "pool", B2[:], TOK[:, 1, hs].unsqueeze(1).broadcast_to([128, 2, 64]), blk3, ALU.mult, [bTOK, bcst], [bB2])
                        TT_(P, "pool", V2[:], TOK[:, 3, hs].unsqueeze(1).broadcast_to([128, 2, 64]), blk3, ALU.mult, [bTOK, bcst], [bV2])
                        TT_(P, "dve", Q2[:], Z[:, h, 64:128].unsqueeze(1).broadcast_to([128, 2, 64]), blk3, ALU.mult, [bZ, bcst], [bQ2])
                        m2.append((B2, bB2, V2, bV2, Q2, bQ2))
                    yield
                    for h in range(2):
                        hs = HS[h]
                        B2, bB2, V2, bV2, Q2, bQ2 = m2[h]
                        A4, bA4 = A4s[h]
                        MM_(P, pMG[hs, 0:128], Z[:, h, 0:64], B2[:].rearrange("p c j -> p (c j)"), True, True, [bZ, bB2], [bpMG])
                        MM_(P, pMG[hs, 128:256], TOK[:, 1, hs], Q2[:].rearrange("p c j -> p (c j)"), True, False, [bTOK, bQ2], [bpMG])
                        MM_(P, pMG[hs, 128:256], TOK[:, 2, hs], V2[:].rearrange("p c j -> p (c j)"), False, True, [bTOK, bV2], [bpMG])
                        MM_(P, pMG[hs, 256:384], Z[:, h, 0:64], A4[:, 128:256], True, True, [bZ, bA4], [bpMG])
                        MM_(P, pMG[hs, 384:512], Z[:, h, 64:128], A4[:, 128:256], True, False, [bZ, bA4], [bpMG])
                        MM_(P, pMG[hs, 384:512], TOK[:, 3, hs], A4[:, 384:512], False, True, [bTOK, bA4], [bpMG])

                def make_seq(cp, MTs, bMTs, Gs, bGs, R2, bR2, Y0, bY0, Yv=Yv, bYn=bYn):
                    def mk(c):
                        def run():
                            ST, bST = stt_[0]
                            cs = slice(c * 64, c * 64 + 64)
                            ps_, bps_ = pSQslots.next()
                            MM_(P, ps_[:, 0:64], ST[:], R2[:, cs], True, True, [bST, bR2], [bps_])
                            c0_ = cp * 128 + c * 64
                            ps2, bps2 = pSQslots.next()
                            MM_(P, ps2, MTs[:, c, :], ST[:], True, True, [bST, bMTs], [bps2])
                            ST2, bST2 = STr.next()
                            TT_(P, "dve", ST2[:], ps2, Gs[:, c, :], ALU.add, [bGs], [bST2, bps2])
                            TT_(P, "dve", Yv[:, c0_:c0_ + 64], ps_[:, 0:64], Y0[:, cs], ALU.add, [bY0], [bYn, bps_])
                            stt_[0] = (ST2, bST2)
                        return run
                    return [mk(0), mk(1)]

                for cp0 in range(0, n // 128, 2):
                    gens = []
                    infos = []
                    for ci_, cp in enumerate((cp0, cp0 + 1)):
                        tr = slice(cp * 128, cp * 128 + 128)
                        ptr_, bptr_ = pool.next()
                        for wi, (src, bsrc) in enumerate([(AhT, bAh), (BtT, bBt), (KtT, bKt), (vS, bvS)]):
                            P.op("pe", lambda e, wi=wi, src=src, tr=tr, ptr_=ptr_: e.transpose(ptr_[:, wi * 128:(wi + 1) * 128], src[:, tr], C("ident", 128)),
                                 reads=[bsrc, bcst], writes=[bptr_])
                        TOK, bTOK = TOKr.next()
                        evac_copy(TOK[:].rearrange("p w k -> p (w k)"), ptr_[:], [], [bTOK, bptr_])
                        pMG, bpMG = pMGs[ci_]
                        infos.append((cp, tr, pMG, bpMG))
                        gens.append(cp_gen(ci_, pMG, bpMG, tr, TOK, bTOK))
                    rounds = 0
                    while gens:
                        for gn in list(gens):
                            try:
                                next(gn)
                            except StopIteration:
                                gens.remove(gn)
                        rounds += 1
                        if rounds >= 2 and pending[0]:
                            pending[0].pop(0)()
                    while pending[0]:
                        pending[0].pop(0)()
                    newp = []
                    for (cp, tr, pMG, bpMG) in infos:
                        MTs, bMTs = MTr.next()
                        Gs, bGs = Gsr.next()
                        for c in range(2):
                            ci = cp * 2 + c
                            for h in range(2):
                                hs = slice(h * 64, h * 64 + 64)
                                STT_(P, MTs[hs, c, hs], cst[hs, CST["ident"] + hs.start:CST["ident"] + hs.stop], WC[hs, ci:ci + 1],
                                     pMG[hs, c * 64:c * 64 + 64], ALU.mult, ALU.add, [bWC, bcst], [bMTs, bpMG])
                                CP_(P, "act", Gs[hs, c, hs], pMG[hs, 128 + c * 64:128 + c * 64 + 64], [], [bGs, bpMG])
                        R2, bR2 = R2r.next()
                        TT_(P, "dve", R2[:], pMG[:, 256:384], RhT[:, tr], ALU.add, [bRh], [bR2, bpMG])
                        Y0, bY0 = Y0r.next()
                        CP_(P, "act", Y0[:], pMG[:, 384:512], [], [bY0, bpMG])
                        newp.extend(make_seq(cp, MTs, bMTs, Gs, bGs, R2, bR2, Y0, bY0))
                    pending[0] = newp
                while pending[0]:
                    pending[0].pop(0)()
                P.dma("sp", lambda e, Yn=Yn, d=d, rows=rows, t0=t0, n=n: e.dma_start(out=YD_[d][0][rows, t0:t0 + n], in_=Yn[:, 0:n]),
                      reads=[bYn], writes=[YD_[d][1]])
    P.barrier()
    kb.st.close()

    kb.st = contextlib.ExitStack()
    o512 = kb.rot_sb(16, [128, 512])
    pO = kb.rot_ps(4)
    gne, bgne = kb.sb([128, 1])
    P.op("pool", lambda e: e.memset(gne[:], GN_EPS), writes=[bgne])
    for (t0, n, slo, shi) in SEGS:
        for cc in range(2):
            cs_ = slice(cc * 128, cc * 128 + 128)
            ya, bya = o512.next()
            yb, byb = o512.next()
            bo, bbo = o512.next()
            gg, bgg = o512.next()
            kb.load(ya[:, 0:n], YD_[0][0][cs_, t0:t0 + n], bya)
            kb.load(yb[:, 0:n], YD_[1][0][cs_, t0:t0 + n], byb, q="act")
            kb.load(bo[:, 0:n], BON_[cs_, t0:t0 + n], bbo)
            kb.load(gg[:, 0:n], G_[cs_, t0:t0 + n], bgg, q="act")
            P.op("pool", lambda e, ya=ya, yb=yb, n=n: e.tensor_tensor(out=ya[:, 0:n], in0=ya[:, 0:n], in1=yb[:, 0:n], op=ALU.add), reads=[bya, byb], writes=[bya])
            pm_, bpm_ = pO.next()
            P.op("pe", lambda e, pm_=pm_, ya=ya, n=n: e.matmul(pm_[:, 0:n], lhsT=C("blk1", 128), rhs=ya[:, 0:n], start=True, stop=True), reads=[bya, bcst], writes=[bpm_])
            yc, byc = o512.next()
            P.op("dve", lambda e, yc=yc, pm_=pm_, ya=ya, n=n: e.scalar_tensor_tensor(
                out=yc[:, 0:n], in0=pm_[:, 0:n], scalar=-1.0 / 64, in1=ya[:, 0:n], op0=ALU.mult, op1=ALU.add), reads=[bpm_, bya], writes=[byc])
            sq, bsq = o512.next()
            P.op("pool", lambda e, sq=sq, yc=yc, n=n: e.tensor_tensor(out=sq[:, 0:n], in0=yc[:, 0:n], in1=yc[:, 0:n], op=ALU.mult), reads=[byc], writes=[bsq])
            pv, bpv = pO.next()
            P.op("pe", lambda e, pv=pv, sq=sq, n=n: e.matmul(pv[:, 0:n], lhsT=C("blk1", 128), rhs=sq[:, 0:n], start=True, stop=True), reads=[bsq, bcst], writes=[bpv])
            rs, brs = o512.next()
            P.op("act", lambda e, rs=rs, pv=pv, n=n: e.activation(out=rs[:, 0:n], in_=pv[:, 0:n], func=AF.Sqrt, bias=gne[:], scale=1.0 / 64), reads=[bpv, bgne], writes=[brs])
            P.op("dve", lambda e, rs=rs, n=n: e.reciprocal(out=rs[:, 0:n], in_=rs[:, 0:n]), reads=[brs], writes=[brs])
            P.op("dve", lambda e, yc=yc, rs=rs, cc=cc, n=n: e.scalar_tensor_tensor(
                out=yc[:, 0:n], in0=yc[:, 0:n], scalar=Vv("lnx_w", cc), in1=rs[:, 0:n], op0=ALU.mult, op1=ALU.mult), reads=[byc, brs, bvec], writes=[byc])
            P.op("dve", lambda e, yc=yc, bo=bo, cc=cc, n=n: e.scalar_tensor_tensor(
                out=yc[:, 0:n], in0=yc[:, 0:n], scalar=Vv("lnx_b", cc), in1=bo[:, 0:n], op0=ALU.add, op1=ALU.add), reads=[byc, bbo, bvec], writes=[byc])
            P.op("pool", lambda e, yc=yc, gg=gg, n=n: e.tensor_tensor(out=yc[:, 0:n], in0=yc[:, 0:n], in1=gg[:, 0:n], op=ALU.mult), reads=[byc, bgg], writes=[byc])
            P.dma("sp", lambda e, yc=yc, cs_=cs_, t0=t0, n=n: e.dma_start(out=g.RWO[cs_, t0:t0 + n], in_=yc[:, 0:n]), reads=[byc], writes=[g.bRWO])
    P.barrier()
    kb.st.close()
    kb.st = main_st


def TT_(P, eng, out, a, b, op, r, w):
    return P.op(eng, lambda e: e.tensor_tensor(out=out, in0=a, in1=b, op=op), reads=r, writes=w)


def TS_(P, eng, out, a, s1, s2, op0, op1, r, w):
    if op1 is None:
        return P.op(eng, lambda e: e.tensor_scalar(out=out, in0=a, scalar1=s1, scalar2=None, op0=op0), reads=r, writes=w)
    return P.op(eng, lambda e: e.tensor_scalar(out=out, in0=a, scalar1=s1, scalar2=s2, op0=op0, op1=op1), reads=r, writes=w)


def STT_(P, out, a, s, b, op0, op1, r, w):
    return P.op("dve", lambda e: e.scalar_tensor_tensor(out=out, in0=a, scalar=s, in1=b, op0=op0, op1=op1), reads=r, writes=w)


def ACT_(P, out, in_, func, r, w, bias=None, scale=1.0):
    if bias is None:
        return P.op("act", lambda e: e.activation(out=out, in_=in_, func=func, scale=scale), reads=r, writes=w)
    return P.op("act", lambda e: e.activation(out=out, in_=in_, func=func, bias=bias, scale=scale), reads=r, writes=w)


def MM_(P, out, lhsT, rhs, start, stop, r, w):
    return P.op("pe", lambda e: e.matmul(out, lhsT=lhsT, rhs=rhs, start=start, stop=stop), reads=r, writes=w)


def CP_(P, eng, out, in_, r, w):
    if eng == "act":
        return P.op("act", lambda e: e.activation(out=out, in_=in_, func=AF.Copy), reads=r, writes=w)
    return P.op(eng, lambda e: e.tensor_copy(out=out, in_=in_), reads=r, writes=w)


def DMA_(P, q, out, in_, r, w):
    return P.dma(q, lambda e: e.dma_start(out=out, in_=in_), reads=r, writes=w)


def MS_(P, eng, ap, val, w):
    return P.op(eng, lambda e: e.memset(ap, val), writes=w)


def rstd8(kb, g, x, bx, n, sq, bsq, pss, bps, rstd, brs):
    P = kb.P
    TT_(P, "pool", sq, x, x, ALU.mult, [bx], [bsq])
    for c in range(8):
        MM_(P, pss, g.ones_bf[:], sq[:, c, :], c == 0, c == 7, [bsq, g.b_ones], [bps])
    ACT_(P, rstd, pss, AF.Sqrt, [g.b_eps], [brs, bps], bias=g.eps[:], scale=1.0 / D)
    P.op("dve", lambda e: e.reciprocal(out=rstd, in_=rstd), reads=[brs], writes=[brs])


def load_w_bf16(kb, dst, bdst_list, src_rows_fn, nrow_chunks, ncols, stg_rot):
    P = kb.P
    for kc in range(nrow_chunks):
        st_, bst = stg_rot.next()
        DMA_(P, "pool" if kc % 2 else "sp", st_[:, 0:ncols], src_rows_fn(kc), [], [bst])
        CP_(P, "act" if kc % 2 else "dve", dst[:, kc, :], st_[:, 0:ncols], [bst], [bdst_list[kc]])


def stage_ada(kb, g):
    P = kb.P
    main = kb.st
    kb.st = contextlib.ExitStack()
    ct, bc = kb.sb([128, 8, 2])
    sc, bsc = kb.sb([128, 8, 2])
    DMA_(P, "sp", ct[:].rearrange("p c r -> p (c r)"), g.cT[:, :], [], [bc])
    ACT_(P, sc[:], ct[:], AF.Silu, [bc], [bsc])
    wr = kb.rot_sb(3, [128, 8, 128])
    pr = kb.rot_ps(2, [128, 8])
    for l in range(g.nl):
        wv = g.ada_w[l].rearrange("(c p) n -> p c n", p=128)
        for j in range(48):
            wt, bw = wr.next()
            DMA_(P, "sp" if j % 2 == 0 else "pool", wt[:], wv[:, :, j * 128:(j + 1) * 128], [], [bw])
            pt, bp = pr.next()
            for kc in range(8):
                MM_(P, pt[:, 0:2], wt[:, kc, :], sc[:, kc, :], kc == 0, kc == 7, [bw, bsc], [bp])
            TS_(P, "dve", g.mt[:, l, j, :], pt[:, 0:2], g.vec[:, l, VEC["ada_b"] + j:VEC["ada_b"] + j + 1], None, ALU.add, None,
                [g.bvec], [g.bmt, bp])
    P.barrier()
    kb.st.close()
    kb.st = main


def mod_ap(g, l, i, c, s):
    return g.mt[:, l, i * 8 + c, s:s + 1]


def norm_mod_tile(kb, g, l, xt, bx, n, seg, gs, bgs, shift_i, xm, bxm, rot):
    P = kb.P
    sq, bsq = rot["sq"].next()
    ps_, bps = rot["pss"].next()
    rs, brs = rot["rs"].next()
    rstd8(kb, g, xt[:, :, 0:n], bx, n, sq[:, :, 0:n], bsq, ps_[:, 0:n], bps, rs[:, 0:n], brs)
    for c in range(8):
        tmp, btmp = rot["tmp"].next()
        STT_(P, tmp[:, 0:n], xt[:, c, 0:n], gs[:, seg, c:c + 1], rs[:, 0:n], ALU.mult, ALU.mult, [bx, bgs, brs], [btmp])
        ACT_(P, xm[:, c, 0:n], tmp[:, 0:n], AF.Identity, [btmp, g.bmt], [bxm], bias=mod_ap(g, l, shift_i, c, seg), scale=1.0)


def make_gs(kb, g, l, gname, scale_i):
    P = kb.P
    gs, bgs = kb.sb([128, 2, 8])
    for s in range(2):
        for c in range(8):
            STT_(P, gs[:, s, c:c + 1], mod_ap(g, l, scale_i, c, s), 1.0, g.vec[:, l, VEC[gname] + c:VEC[gname] + c + 1], ALU.add, ALU.mult,
                 [g.bmt, g.bvec], [bgs])
    return gs, bgs


def make_gp(kb, g, l, gname, gate_i):
    P = kb.P
    gp, bgp = kb.sb([128, 2, 8])
    for s in range(2):
        for c in range(8):
            TT_(P, "dve", gp[:, s, c:c + 1], mod_ap(g, l, gate_i, c, s), g.vec[:, l, VEC[gname] + c:VEC[gname] + c + 1], ALU.mult,
                [g.bmt, g.bvec], [bgp])
    return gp, bgp


def xsrc_tile(g, l, t0, n):
    if t0 < CTX:
        src = g.hT_in if l == 0 else g.HT
        return src.rearrange("(c p) t -> p c t", p=128)[:, :, t0:t0 + n]
    src = g.xT_in if l == 0 else g.XT
    return src.rearrange("(c p) t -> p c t", p=128)[:, :, t0 - CTX:t0 - CTX + n]


def xdst_tile(g, l, t0, n, final):
    if t0 < CTX:
        return g.HT.rearrange("(c p) t -> p c t", p=128)[:, :, t0:t0 + n]
    dst = g.outT if final else g.XT
    return dst.rearrange("(c p) t -> p c t", p=128)[:, :, t0 - CTX:t0 - CTX + n]


def stage_k1(kb, g, l):
    P = kb.P
    main = kb.st
    kb.st = contextlib.ExitStack()
    NJ = N_IN // 128
    wbf, _ = kb.sb([128, 8, N_IN], BF16)
    bwk = [Buf() for _ in range(8)]
    wst = kb.rot_sb(2, [128, N_IN], F32)
    wv = g.w_in[l].rearrange("(c p) n -> p c n", p=128)
    load_w_bf16(kb, wbf, bwk, lambda kc: wv[:, kc, :], 8, N_IN, wst)
    gs, bgs = make_gs(kb, g, l, "g_mix_pre", 1)
    rot = {"sq": kb.rot_sb(1, [128, 8, 512], BF16), "pss": kb.rot_ps(1), "rs": kb.rot_sb(2, [128, 512]), "tmp": kb.rot_sb(2, [128, 512])}
    xr = kb.rot_sb(2, [128, 8, 512])
    xmr = kb.rot_sb(2, [128, 8, 512], BF16)
    pmm = kb.rot_ps(4)
    otr = kb.rot_sb(4, [128, 512])
    ev = 0

    def prep(idx):
        (t0, n, slo, shi) = SEGS[idx]
        seg = 1 if t0 < CTX else 0
        xt, bx = xr.next()
        DMA_(P, "sp", xt[:, :, 0:n], xsrc_tile(g, l, t0, n), [g.bX], [bx])
        xm, bxm = xmr.next()
        norm_mod_tile(kb, g, l, xt, bx, n, seg, gs, bgs, 0, xm, bxm, rot)
        return (t0, n, xm, bxm)

    cur = prep(0)
    for idx in range(len(SEGS)):
        nxt = prep(idx + 1) if idx + 1 < len(SEGS) else None
        (t0, n, xm, bxm) = cur
        for j in range(NJ):
            pt, bp = pmm.next()
            for c in range(8):
                MM_(P, pt[:, 0:n], wbf[:, c, j * 128:(j + 1) * 128], xm[:, c, 0:n], c == 0, c == 7, [bwk[c], bxm], [bp])
            ot, bo = otr.next()
            CP_(P, "dve" if ev % 2 == 0 else "act", ot[:, 0:n], pt[:, 0:n], [], [bo, bp])
            ev += 1
            DMA_(P, "sp", g.UT[j * 128:(j + 1) * 128, t0:t0 + n], ot[:, 0:n], [bo], [g.bUT])
        cur = nxt
    P.barrier()
    kb.st.close()
    kb.st = main


def stage_k2(kb, g, l):
    P = kb.P
    main = kb.st
    kb.st = contextlib.ExitStack()

    def C(name, n, rows=slice(0, 128)):
        return g.cst[rows, CST[name]:CST[name] + n]

    QR, bQR = kb.sb([128, 2, TT], BF16)
    KR, bKR = kb.sb([128, TT], BF16)
    vaug, bva = kb.sb([128, 34, 128], BF16)
    MS_(P, "pool", vaug[:], 1.0, [bva])
    xin = kb.rot_sb(4, [128, 512])
    csr = kb.rot_sb(3, [128, 2, 512])
    t5 = kb.rot_sb(16, [128, 512])
    pq0 = kb.rot_ps(1)
    vtr = kb.rot_sb(4, [64, 128])
    psT = kb.rot_ps(5)
    pq = Rot(pq0.items + psT.items)
    poT = kb.rot_ps(2)
    ptr = kb.rot_sb(5, [128, 512], BF16)
    rdr = kb.rot_sb(2, [64, 512])
    oor = kb.rot_sb(2, [64, 512])
    gne, bgne = kb.sb([128, 1])
    MS_(P, "pool", gne[:], NORM_EPS, [bgne])

    def normrope(src_rows, gain_name, dst_fn, bdst):
        for (t0, n, slo, shi) in SEGS:
            x, bx = xin.next()
            for i, (ps_, r0) in enumerate(src_rows):
                DMA_(P, "sp" if i == 0 else "act", x[ps_, 0:n], g.UT[r0:r0 + (ps_.stop - ps_.start), t0:t0 + n], [g.bUT], [bx])
            sq, bsq = t5.next()
            TT_(P, "pool", sq[:, 0:n], x[:, 0:n], x[:, 0:n], ALU.mult, [bx], [bsq])
            pa, bpa = pq.next()
            MM_(P, pa[:, 0:n], C("blk1", 128), sq[:, 0:n], True, True, [bsq, g.bcst], [bpa])
            rs, brs = t5.next()
            ACT_(P, rs[:, 0:n], pa[:, 0:n], AF.Sqrt, [bgne], [brs, bpa], bias=gne[:], scale=1.0 / 64)
            P.op("dve", lambda e, rs=rs, n=n: e.reciprocal(out=rs[:, 0:n], in_=rs[:, 0:n]), reads=[brs], writes=[brs])
            xn, bxn = t5.next()
            STT_(P, xn[:, 0:n], x[:, 0:n], g.vec[:, l, VEC[gain_name]:VEC[gain_name] + 1], rs[:, 0:n], ALU.mult, ALU.mult, [bx, brs, g.bvec], [bxn])
            if t0 < CTX:
                CP_(P, "act", dst_fn(t0, n), xn[:, 0:n], [bxn], [bdst])
            else:
                cs_, bcs = csr.next()
                DMA_(P, "sp", cs_[:, 0, 0:n], g.cosT[:, t0 - CTX:t0 - CTX + n], [], [bcs])
                DMA_(P, "act", cs_[:, 1, 0:n], g.sinT[:, t0 - CTX:t0 - CTX + n], [], [bcs])
                pb, bpb = pq.next()
                MM_(P, pb[:, 0:n], C("prot", 128), xn[:, 0:n], True, True, [bxn, g.bcst], [bpb])
                t1, bt1 = t5.next()
                TT_(P, "pool", t1[:, 0:n], xn[:, 0:n], cs_[:, 0, 0:n], ALU.mult, [bxn, bcs], [bt1])
                t2, bt2 = t5.next()
                TT_(P, "dve", t2[:, 0:n], pb[:, 0:n], cs_[:, 1, 0:n], ALU.mult, [bcs], [bt2, bpb])
                TT_(P, "dve", dst_fn(t0, n), t1[:, 0:n], t2[:, 0:n], ALU.add, [bt1, bt2], [bdst])

    for gq in range(2):
        for hp2 in range(2):
            normrope([(slice(0, 128), gq * 256 + hp2 * 128)], "qn", lambda t0, n, hp2=hp2: QR[:, hp2, t0:t0 + n], bQR)
        kr0 = 512 + gq * 64
        normrope([(slice(0, 64), kr0), (slice(64, 128), kr0)], "kn", lambda t0, n: KR[:, t0:t0 + n], bKR)
        vr0 = 640 + gq * 64
        for kc in range(34):
            vt, bvt = vtr.next()
            DMA_(P, "sp" if kc % 2 == 0 else "act", vt[:], g.UT[vr0:vr0 + 64, kc * 128:(kc + 1) * 128], [g.bUT], [bvt])
            pv, bpv = pq.next()
            P.op("pe", lambda e, pv=pv, vt=vt: e.transpose(pv[:, 0:64], vt[:], g.cst[0:64, CST["ident"]:CST["ident"] + 64]),
                 reads=[bvt, g.bcst], writes=[bpv])
            CP_(P, "dve", vaug[:, kc, 0:64], pv[:, 0:64], [], [bva, bpv])
        for hh in range(4):
            hsl = slice((hh % 2) * 64, (hh % 2) * 64 + 64)
            hp2 = hh // 2
            row0 = (gq * 4 + hh) * 64
            for (t0, n, slo, shi) in SEGS:
                nk = 2 if t0 < CTX else 34
                po, bpo = poT.next()
                sts = {}
                LOOK = 4

                def issue_s(kc, n=n, t0=t0):
                    ps_, bps = psT.next()
                    MM_(P, ps_[:, 0:n], KR[hsl, kc * 128:(kc + 1) * 128], QR[hsl, hp2, t0:t0 + n], True, True, [bKR, bQR], [bps])
                    sts[kc] = (ps_, bps)
                for kc in range(min(LOOK, nk)):
                    issue_s(kc)
                for kc in range(nk):
                    ps_, bps = sts.pop(kc)
                    pt_, bpt = ptr.next()
                    ACT_(P, pt_[:, 0:n], ps_[:, 0:n], AF.Exp, [], [bpt, bps], scale=0.125)
                    if kc + LOOK < nk:
                        issue_s(kc + LOOK)
                    MM_(P, po[:, 0:n], vaug[:, kc, :], pt_[:, 0:n], kc == 0, kc == nk - 1, [bva, bpt], [bpo])
                rd, brd = rdr.next()
                P.op("dve", lambda e, rd=rd, po=po, n=n: e.reciprocal(out=rd[:, 0:n], in_=po[64:128, 0:n]), reads=[], writes=[brd, bpo])
                oo, boo = oor.next()
                TT_(P, "dve", oo[:, 0:n], po[0:64, 0:n], rd[:, 0:n], ALU.mult, [brd], [boo, bpo])
                DMA_(P, "sp", g.ATT[row0:row0 + 64, t0:t0 + n], oo[:, 0:n], [boo], [g.bATT])
    P.barrier()
    kb.st.close()
    kb.st = main


def epilogue(kb, g, l, src, bsrc, n, seg, t0, gp, bgp, rot, final):
    P = kb.P
    sq, bsq = rot["sq"].next()
    ps_, bps = rot["pss"].next()
    rs, brs = rot["rs"].next()
    rstd8(kb, g, src[:, :, 0:n], bsrc, n, sq[:, :, 0:n], bsq, ps_[:, 0:n], bps, rs[:, 0:n], brs)
    xt, bx = rot["x"].next()
    DMA_(P, "act", xt[:, :, 0:n], rot["xsrc"](t0, n), [g.bX], [bx])
    for j in range(8):
        STT_(P, src[:, j, 0:n], src[:, j, 0:n], gp[:, seg, j:j + 1], rs[:, 0:n], ALU.mult, ALU.mult, [bsrc, bgp, brs], [bsrc])
    TT_(P, "pool", xt[:, :, 0:n], xt[:, :, 0:n], src[:, :, 0:n], ALU.add, [bx, bsrc], [bx])
    DMA_(P, "sp", xdst_tile(g, l, t0, n, final), xt[:, :, 0:n], [bx], [g.bX2])


def stage_k4(kb, g, l):
    P = kb.P
    main = kb.st
    kb.st = contextlib.ExitStack()
    wbf, _ = kb.sb([128, 8, D], BF16)
    bwk = [Buf() for _ in range(8)]
    wst = kb.rot_sb(2, [128, D], F32)
    wv = g.w_out[l].rearrange("(c p) n -> p c n", p=128)
    load_w_bf16(kb, wbf, bwk, lambda kc: wv[:, kc, :], 8, D, wst)
    gp, bgp = make_gp(kb, g, l, "g_mix_post", 2)
    rot = {"sq": kb.rot_sb(1, [128, 8, 512], BF16), "pss": kb.rot_ps(1), "rs": kb.rot_sb(2, [128, 512]),
           "x": kb.rot_sb(2, [128, 8, 512]), "xsrc": lambda t0, n: xsrc_tile(g, l, t0, n)}
    mixr = kb.rot_sb(1, [128, 6, 512])
    cvr = kb.rot_sb(2, [128, 6, 514])
    pr_ = kb.rot_sb(2, [128, 2, 514])
    cnr = kb.rot_sb(2, [128, 2, 512])
    mbr = kb.rot_sb(2, [128, 8, 512], BF16)
    mor = kb.rot_sb(2, [128, 8, 512])
    pmm = kb.rot_ps(4)
    UTcv = g.UT[1920:2688, :].rearrange("(c p) t -> p c t", p=128)
    ATTv = g.ATT.rearrange("(c p) t -> p c t", p=128)
    RWOv = g.RWO.rearrange("(c p) t -> p c t", p=128)
    ev = 0
    for (t0, n, slo, shi) in SEGS:
        seg = 1 if t0 < CTX else 0
        mx, bmx = mixr.next()
        DMA_(P, "sp", mx[:, 0:4, 0:n], ATTv[:, :, t0:t0 + n], [g.bATT], [bmx])
        DMA_(P, "act", mx[:, 4:6, 0:n], RWOv[:, :, t0:t0 + n], [g.bRWO], [bmx])
        cv, bcv = cvr.next()
        MS_(P, "pool", cv[:, :, 0:1], 0.0, [bcv])
        MS_(P, "pool", cv[:, :, n + 1:n + 2], 0.0, [bcv])
        lo, hi = max(t0 - 1, slo), min(t0 + n + 1, shi)
        DMA_(P, "sp", cv[:, :, lo - (t0 - 1):hi - (t0 - 1)], UTcv[:, :, lo:hi], [g.bUT], [bcv])
        pp, bpp = pr_.next()
        TT_(P, "pool", pp[:, :, 0:n + 2], cv[:, 2:4, 0:n + 2], cv[:, 4:6, 0:n + 2], ALU.mult, [bcv], [bpp])
        cn, bcn = cnr.next()
        mb, bmb = mbr.next()
        for cc in range(2):
            sv = lambda i, cc=cc: g.vec[:, l, VEC["sconv"] + i * 2 + cc:VEC["sconv"] + i * 2 + cc + 1]
            TS_(P, "dve", cn[:, cc, 0:n], pp[:, cc, 1:n + 1], sv(1), None, ALU.mult, None, [bpp, g.bvec], [bcn])
            STT_(P, cn[:, cc, 0:n], pp[:, cc, 0:n], sv(0), cn[:, cc, 0:n], ALU.mult, ALU.add, [bpp, g.bvec, bcn], [bcn])
            STT_(P, cn[:, cc, 0:n], pp[:, cc, 2:n + 2], sv(2), cn[:, cc, 0:n], ALU.mult, ALU.add, [bpp, g.bvec, bcn], [bcn])
            TT_(P, "dve", mb[:, 6 + cc, 0:n], cn[:, cc, 0:n], cv[:, cc, 1:n + 1], ALU.mult, [bcn, bcv], [bmb])
        CP_(P, "act", mb[:, 0:6, 0:n], mx[:, :, 0:n], [bmx], [bmb])
        mo, bmo = mor.next()
        for j in range(8):
            pt, bp = pmm.next()
            for c in range(8):
                MM_(P, pt[:, 0:n], wbf[:, c, j * 128:(j + 1) * 128], mb[:, c, 0:n], c == 0, c == 7, [bwk[c], bmb], [bp])
            CP_(P, "dve" if ev % 2 == 0 else "act", mo[:, j, 0:n], pt[:, 0:n], [], [bmo, bp])
            ev += 1
        epilogue(kb, g, l, mo, bmo, n, seg, t0, gp, bgp, rot, False)
    P.barrier()
    kb.st.close()
    kb.st = main


NPAD = TT + 3


def pad_col(t):
    return t + 1 if t < CTX else t + 2


def stage_k5(kb, g, l, final):
    P = kb.P
    main = kb.st
    kb.st = contextlib.ExitStack()
    gs, bgs = make_gs(kb, g, l, "g_ffn_pre", 4)
    hres, bh = kb.sb([128, 8, TT], BF16)
    XM_src = lambda t0, n: (g.HT if t0 < CTX else g.XT).rearrange("(c p) t -> p c t", p=128)[:, :, (t0 if t0 < CTX else t0 - CTX):(t0 if t0 < CTX else t0 - CTX) + n]
    stk = contextlib.ExitStack()
    outer = kb.st
    kb.st = stk
    rot = {"sq": kb.rot_sb(1, [128, 8, 512], BF16), "pss": kb.rot_ps(1), "rs": kb.rot_sb(2, [128, 512]), "tmp": kb.rot_sb(2, [128, 512])}
    xr = kb.rot_sb(2, [128, 8, 512])
    for (t0, n, slo, shi) in SEGS:
        seg = 1 if t0 < CTX else 0
        xt, bx = xr.next()
        DMA_(P, "sp", xt[:, :, 0:n], XM_src(t0, n), [g.bX2], [bx])

        class _V:
            pass
        hv = hres[:, :, t0:t0 + n]
        norm_mod_tile(kb, g, l, xt, bx, n, seg, gs, bgs, 3, hv, bh, rot)
    P.barrier()
    stk.close()
    kb.st = outer
    gar = kb.rot_sb(2, [128, NPAD], BF16)
    upr = kb.rot_sb(2, [128, NPAD], BF16)
    tgr = kb.rot_sb(2, [128, NPAD])
    sgr = kb.rot_sb(1, [128, NPAD], BF16)
    for tl, bt in gar.items + upr.items:
        MS_(P, "pool", tl[:, 0:1], 0.0, [bt])
        MS_(P, "pool", tl[:, CTX + 1:CTX + 2], 0.0, [bt])
        MS_(P, "pool", tl[:, NPAD - 1:NPAD], 0.0, [bt])
    actr = kb.rot_sb(1, [128, NPAD], BF16)
    wst = kb.rot_sb(2, [128, 8, 256], F32)
    wbr = kb.rot_sb(2, [128, 8, 256], BF16)
    pmm = kb.rot_ps(4)
    upv = g.ffn_up[l].rearrange("(c p) n -> p c n", p=128)
    N1 = NPAD - 2
    ev = 0
    for j in range(22):
        ws, bws = wst.next()
        DMA_(P, "sp", ws[:, :, 0:128], upv[:, :, j * 128:(j + 1) * 128], [], [bws])
        DMA_(P, "pool", ws[:, :, 128:256], upv[:, :, D_FF + j * 128:D_FF + (j + 1) * 128], [], [bws])
        wb, bwb = wbr.next()
        CP_(P, "pool", wb[:], ws[:], [bws], [bwb])
        gaT, bga = gar.next()
        upT, bup = upr.next()
        tg, btg = tgr.next()
        sg_, bsg_ = sgr.next()
        for (t0, n, slo, shi) in SEGS:
            pc = pad_col(t0)
            for which, (dst, bdst) in enumerate(((gaT, bga), (upT, bup))):
                pt, bp = pmm.next()
                for c in range(8):
                    MM_(P, pt[:, 0:n], wb[:, c, which * 128:(which + 1) * 128], hres[:, c, t0:t0 + n], c == 0, c == 7, [bwb, bh], [bp])
                CP_(P, "act", dst[:, pc:pc + n], pt[:, 0:n], [], [bdst, bp])
                ev += 1
        def conv3(src, bsrc, which, j=j):
            fv = lambda i: g.vec[:, l, VEC["fconv"] + i * 44 + which * 22 + j:VEC["fconv"] + i * 44 + which * 22 + j + 1]
            TS_(P, "pool", tg[:, 0:N1], src[:, 1:N1 + 1], fv(1), 0.0, ALU.mult, ALU.add, [bsrc, g.bvec], [btg])
            STT_(P, tg[:, 0:N1], src[:, 0:N1], fv(0), tg[:, 0:N1], ALU.mult, ALU.add, [bsrc, g.bvec, btg], [btg])
            STT_(P, tg[:, 0:N1], src[:, 2:N1 + 2], fv(2), tg[:, 0:N1], ALU.mult, ALU.add, [bsrc, g.bvec, btg], [btg])
        conv3(gaT, bga, 0)
        ACT_(P, sg_[:, 0:N1], tg[:, 0:N1], AF.Silu, [btg], [bsg_])
        conv3(upT, bup, 1)
        ac, bac = actr.next()
        TT_(P, "dve", ac[:, 0:N1], sg_[:, 0:N1], tg[:, 0:N1], ALU.mult, [bsg_, btg], [bac])
        DMA_(P, "sp", g.ACTS[j * 128:(j + 1) * 128, 0:CTX], ac[:, 0:CTX], [bac], [g.bACTS])
        DMA_(P, "act", g.ACTS[j * 128:(j + 1) * 128, CTX:TT], ac[:, CTX + 1:CTX + 1 + SEQ], [bac], [g.bACTS])
    P.barrier()
    kb.st.close()
    kb.st = contextlib.ExitStack()
    wd, _ = kb.sb([128, 22, D], BF16)
    bwd = [Buf() for _ in range(22)]
    wst2 = kb.rot_sb(2, [128, D], F32)
    dv = g.ffn_down[l].rearrange("(j p) n -> p j n", p=128)
    load_w_bf16(kb, wd, bwd, lambda kc: dv[:, kc, :], 22, D, wst2)
    gp, bgp = make_gp(kb, g, l, "g_ffn_post", 5)
    rot = {"sq": kb.rot_sb(1, [128, 8, 512], BF16), "pss": kb.rot_ps(1), "rs": kb.rot_sb(2, [128, 512]),
           "x": kb.rot_sb(2, [128, 8, 512]), "xsrc": XM_src}
    atr = kb.rot_sb(2, [128, 22, 512], BF16)
    fr = kb.rot_sb(2, [128, 8, 512])
    pmm = kb.rot_ps(4)
    AV = g.ACTS.rearrange("(j p) t -> p j t", p=128)
    ev = 0
    g.bX, g.bX2 = g.bX2, g.bX
    for (t0, n, slo, shi) in SEGS:
        seg = 1 if t0 < CTX else 0
        at, bat = atr.next()
        DMA_(P, "sp", at[:, :, 0:n], AV[:, :, t0:t0 + n], [g.bACTS], [bat])
        f, bf = fr.next()
        for nn in range(8):
            pt, bp = pmm.next()
            for j in range(22):
                MM_(P, pt[:, 0:n], wd[:, j, nn * 128:(nn + 1) * 128], at[:, j, 0:n], j == 0, j == 21, [bwd[j], bat], [bp])
            CP_(P, "dve" if ev % 2 == 0 else "act", f[:, nn, 0:n], pt[:, 0:n], [], [bf, bp])
            ev += 1
        epilogue(kb, g, l, f, bf, n, seg, t0, gp, bgp, rot, final)
    g.bX, g.bX2 = g.bX2, g.bX
    P.barrier()
    kb.st.close()
    kb.st = main


def build_mega(nl=DEPTH, dbg=()):
    kb = KB()
    P = kb.P
    nc = kb.nc
    g = G()
    g.nl = nl
    g.scr = {}
    g.xT_in = kb.din("xT", [D, SEQ])
    g.hT_in = kb.din("hT", [D, CTX])
    g.cT = kb.din("cT", [128, 16])
    cst_d = kb.din("cst", [128, NCST])
    vec_d = kb.din("vec", [128, DEPTH * NV])
    g.cosT = kb.din("cosT", [128, SEQ])
    g.sinT = kb.din("sinT", [128, SEQ])
    g.ada_w = kb.din("ada_w", [DEPTH, D, 6 * D])
    g.w_in = kb.din("w_in", [DEPTH, D, N_IN])
    g.w_out = kb.din("w_out", [DEPTH, D, D])
    g.ffn_up = kb.din("ffn_up", [DEPTH, D, 2 * D_FF])
    g.ffn_down = kb.din("ffn_down", [DEPTH, D_FF, D])
    g.w_up = kb.din("w_up", [DEPTH, 128, 256])
    g.a_up = kb.din("a_up", [DEPTH, 128, 256])
    g.g_up = kb.din("g_up", [DEPTH, 128, 256])
    g.outT = kb.dout("outT", [D, SEQ])

    def scr(name, shape, dt=F32):
        return nc.dram_tensor(name, list(shape), dt, kind="Internal").ap()

    g.UT, g.bUT = scr("UT", [N_IN, TT]), Buf()
    g.ATT, g.bATT = scr("ATT", [512, TT]), Buf()
    g.RWO, g.bRWO = scr("RWO", [256, TT]), Buf()
    g.XT = scr("XT", [D, SEQ])
    g.HT = scr("HT", [D, CTX])
    g.bX, g.bX2 = Buf(), Buf()
    g.ACTS, g.bACTS = scr("ACTS", [D_FF, TT], BF16), Buf()
    g.cst, g.bcst = kb.sb([128, NCST])
    g.vec, g.bvec = kb.sb([128, DEPTH, NV])
    g.mt, g.bmt = kb.sb([128, DEPTH, 48, 2])
    g.ones_bf, g.b_ones = kb.sb([128, 128], BF16)
    g.eps, g.b_eps = kb.sb([128, 1])
    DMA_(P, "sp", g.cst[:], cst_d[:, :], [], [g.bcst])
    DMA_(P, "sp", g.vec[:].rearrange("p l v -> p (l v)"), vec_d[:, :], [], [g.bvec])
    MS_(P, "pool", g.ones_bf[:], 1.0, [g.b_ones])
    MS_(P, "pool", g.eps[:], NORM_EPS, [g.b_eps])
    stage_ada(kb, g)
    for l in range(nl):
        final = (l == nl - 1)
        stage_k1(kb, g, l)
        stage_k2(kb, g, l)
        stage_k3(kb, g, l)
        stage_k4(kb, g, l)
        stage_k5(kb, g, l, final)
    for name in dbg:
        src = getattr(g, name)
        o = kb.dout("o_" + name, list(src.shape), src.dtype)
        kb.outs.append(DMA_(P, "sp", o, src, [], []))
    for k_, v_ in g.bX.w.items():
        kb.outs.append((k_, v_))
    for k_, v_ in g.bX2.w.items():
        kb.outs.append((k_, v_))
    return kb.finish()


def make_vec_np(I):
    v = np.zeros((128, DEPTH, NV), np.float32)
    for l in range(DEPTH):
        def put(name, arr):
            v[:, l, VEC[name]:VEC[name] + arr.shape[1]] = arr
        mu = I["rwkv_mu"][l]
        put("mu", np.stack([fm(mu[0], 9), fm(mu[1], 9)], 2).reshape(128, 18))
        put("w0", np.stack([fm(I["rwkv_w0"][l][d], 2) for d in range(2)], 1).reshape(128, 4))
        put("a0", np.stack([fm(I["rwkv_a0"][l][d], 2) for d in range(2)], 1).reshape(128, 4))
        put("k_k", fm(I["rwkv_k_k"][l], 2))
        put("k_a", fm(I["rwkv_k_a"][l], 2))
        put("r_k", fm(I["rwkv_r_k"][l].reshape(256), 2))
        put("lnx_w", fm(I["rwkv_lnx_w"][l], 2))
        put("lnx_b", fm(I["rwkv_lnx_b"][l], 2))
        put("g_mix_pre", fm(I["norm_mix_pre"][l], 8))
        put("g_mix_post", fm(I["norm_mix_post"][l], 8))
        put("g_ffn_pre", fm(I["norm_ffn_pre"][l], 8))
        put("g_ffn_post", fm(I["norm_ffn_post"][l], 8))
        put("qn", np.tile(I["q_norm"][l], 2)[:, None])
        put("kn", np.tile(I["k_norm"][l], 2)[:, None])
        put("sconv", np.stack([fm(I["sconv_w"][l][i], 2) for i in range(3)], 1).reshape(128, 6))
        put("fconv", np.stack([fm(I["ffn_conv"][l][i], 44) for i in range(3)], 1).reshape(128, 132))
        put("ada_b", fm(I["ada_b"][l], 48))
    return v.reshape(128, DEPTH * NV)


def rope_tables():
    t = np.arange(SEQ)
    row = (t // 64).astype(np.float32)
    col = (t % 64).astype(np.float32)
    inv = (np.float32(10000.0) ** (-np.arange(16, dtype=np.float32) / np.float32(16))).astype(np.float32)
    ang = np.concatenate([row[:, None] * inv, col[:, None] * inv], -1).astype(np.float32)
    cos = np.cos(ang).astype(np.float32).T
    sin = np.sin(ang).astype(np.float32).T
    cosf = np.concatenate([cos, cos, cos, cos], 0)
    sinf = np.concatenate([sin, sin, sin, sin], 0)
    return np.ascontiguousarray(cosf), np.ascontiguousarray(sinf)


def make_maps(I):
    vec = make_vec_np(I)
    cst = make_cst_np()
    cosf, sinf = rope_tables()
    shared = {"cst": cst, "vec": vec, "cosT": cosf, "sinT": sinf,
              "ada_w": np.asarray(I["ada_w"], np.float32), "w_in": np.asarray(I["w_in"], np.float32),
              "w_out": np.asarray(I["w_out"], np.float32), "ffn_up": np.asarray(I["ffn_up"], np.float32),
              "ffn_down": np.asarray(I["ffn_down"], np.float32),
              "w_up": np.ascontiguousarray(np.asarray(I["rwkv_w_up"], np.float32).reshape(DEPTH, 128, 256)),
              "a_up": np.ascontiguousarray(np.asarray(I["rwkv_a_up"], np.float32).reshape(DEPTH, 128, 256)),
              "g_up": np.asarray(I["rwkv_g_up"], np.float32)}
    maps = []
    for b in range(BATCH):
        c2 = np.stack([I["c"][b], I["c_ctx"]], 1).astype(np.float32)
        cT = np.ascontiguousarray(c2.reshape(8, 128, 2).transpose(1, 0, 2)).reshape(128, 16)
        m = dict(shared)
        m["xT"] = np.ascontiguousarray(np.asarray(I["x"][b], np.float32).T)
        m["hT"] = np.ascontiguousarray(np.asarray(I["ctx"][b], np.float32).T)
        m["cT"] = cT
        maps.append(m)
    return maps


def kernel(**inputs):
    I = {k: np.asarray(v) for k, v in inputs.items()}
    nc = cached("mega", build_mega)
    maps = make_maps(I)
    res = run_bass_kernel_spmd(nc, maps, core_ids=list(range(BATCH))).results
    out = np.stack([np.ascontiguousarray(res[b]["outT"].T) for b in range(BATCH)], 0)
    return out.astype(np.float32)
```
